# Optimizing a Trainium2 kernel written in Bass

```python
import math
import jax, jax.numpy as jnp
from jax import lax
import numpy as np

D_MODEL = 2048
BATCH = 1
SEQ = 8192
DEPTH = 2

N_BRANCH = 4
BRANCH_WIDTH = D_MODEL // 4
SC_TAPS = 3
CONF_TAPS = 31
POOL_WINDOWS = (2, 4, 8, 16)
POOL_GROUP = BRANCH_WIDTH // len(POOL_WINDOWS)
SB_HEADS = 8
SB_HEAD_DIM = BRANCH_WIDTH // SB_HEADS
Q_BLOCK = 128
D_FF = 5632
N_IN_SLOTS = 9
EPS = 1e-6

kernel_name = "hybrid_gated_conv_pool_stickbreaking_block"


def rms_norm(x, g):
    xf = x.astype(jnp.float32)
    y = xf * lax.rsqrt(jnp.mean(xf * xf, axis=-1, keepdims=True) + EPS)
    return (y * g.astype(jnp.float32)).astype(x.dtype)


def layer_norm(x, g, b):
    xf = x.astype(jnp.float32)
    mu = jnp.mean(xf, axis=-1, keepdims=True)
    xc = xf - mu
    y = xc * lax.rsqrt(jnp.mean(xc * xc, axis=-1, keepdims=True) + EPS)
    return (y * g.astype(jnp.float32) + b.astype(jnp.float32)).astype(x.dtype)


def swiglu(h, w1, w3, w2):
    return (jax.nn.silu(h @ w1) * (h @ w3)) @ w2


def causal_depthwise_conv(u, w):
    taps, c = w.shape
    return lax.conv_general_dilated(
        u, w[:, None, :].astype(u.dtype), window_strides=(1,), padding=[(taps - 1, 0)],
        dimension_numbers=("NWC", "WIO", "NWC"), feature_group_count=c)


def multiscale_pool(u):
    b, s, c = u.shape
    uf = u.astype(jnp.float32)
    csum = jnp.cumsum(uf, axis=1)
    csp = jnp.concatenate([jnp.zeros((b, 1, c), jnp.float32), csum], axis=1)
    pos = jnp.arange(s, dtype=jnp.int32)
    outs = []
    for g, w in enumerate(POOL_WINDOWS):
        sl = slice(g * POOL_GROUP, (g + 1) * POOL_GROUP)
        lo = jnp.maximum(pos + 1 - w, 0)
        cnt = jnp.minimum(pos + 1, w).astype(jnp.float32)[None, :, None]
        win = csum[:, :, sl] - csp[:, lo, sl]
        outs.append(win / cnt - uf[:, :, sl])
    return jnp.concatenate(outs, axis=-1).astype(u.dtype)


def stick_breaking_attention(q, k, v):
    b, s, h, hd = q.shape
    nb = s // Q_BLOCK
    kt = k.transpose(0, 2, 1, 3)
    vt = v.transpose(0, 2, 1, 3)
    qb = q.reshape(b, nb, Q_BLOCK, h, hd).transpose(1, 0, 3, 2, 4)
    starts = jnp.arange(nb, dtype=jnp.int32) * Q_BLOCK
    key_pos = jnp.arange(s, dtype=jnp.int32)
    scale = 1.0 / math.sqrt(hd)

    def block(args):
        q_blk, start = args
        z = jnp.einsum("bhqd,bhkd->bhqk", q_blk, kt,
                       preferred_element_type=jnp.float32) * scale
        qpos = start + jnp.arange(Q_BLOCK, dtype=jnp.int32)
        mask = key_pos[None, :] < qpos[:, None]
        log_1mb = jnp.where(mask, jax.nn.log_sigmoid(-z), 0.0)
        between = lax.cumsum(log_1mb, axis=3, reverse=True) - log_1mb
        a = jnp.where(mask, jnp.exp(jax.nn.log_sigmoid(z) + between), 0.0)
        return jnp.einsum("bhqk,bhkd->bhqd", a.astype(vt.dtype), vt)

    out = lax.map(block, (qb, starts))
    return out.transpose(1, 0, 3, 2, 4).reshape(b, s, h * hd)


def setup_inputs(seed: int = 0) -> dict:
    key = jax.random.key(seed)
    ks = jax.random.split(key, 24)
    L, D, W, F, G, HD = DEPTH, D_MODEL, BRANCH_WIDTH, D_FF, POOL_GROUP, SB_HEAD_DIM

    def nrm(k, shape, fan_in):
        return jax.random.normal(k, shape, jnp.float32) * (fan_in ** -0.5)

    def gain(k, shape):
        return 1.0 + 0.05 * jax.random.normal(k, shape, jnp.float32)

    return {
        "x": jax.random.normal(ks[0], (BATCH, SEQ, D), jnp.float32),
        "ffn1_norm": gain(ks[1], (L, D)),
        "ffn1_w1": nrm(ks[2], (L, D, F), D),
        "ffn1_w3": nrm(ks[3], (L, D, F), D),
        "ffn1_w2": nrm(ks[4], (L, F, D), F),
        "mix_norm": gain(ks[5], (L, D)),
        "w_in": nrm(ks[6], (L, D, N_IN_SLOTS * W), D),
        "conv_a": nrm(ks[7], (L, SC_TAPS, W), SC_TAPS),
        "conv_b": nrm(ks[8], (L, CONF_TAPS, W), CONF_TAPS),
        "ln_b_gain": gain(ks[9], (L, W)),
        "ln_b_bias": 0.02 * jax.random.normal(ks[10], (L, W), jnp.float32),
        "pool_map": nrm(ks[11], (L, len(POOL_WINDOWS), G, G), G),
        "pool_scale": gain(ks[12], (L, W)),
        "q_norm": gain(ks[13], (L, HD)),
        "k_norm": gain(ks[14], (L, HD)),
        "w_branch": nrm(ks[15], (L, N_BRANCH, W, D), W),
        "w_gate": nrm(ks[16], (L, D, N_BRANCH * D), D),
        "w_out": nrm(ks[17], (L, D, D), D),
        "ffn2_norm": gain(ks[18], (L, D)),
        "ffn2_w1": nrm(ks[19], (L, D, F), D),
        "ffn2_w3": nrm(ks[20], (L, D, F), D),
        "ffn2_w2": nrm(ks[21], (L, F, D), F),
    }


def reference(x, ffn1_norm, ffn1_w1, ffn1_w3, ffn1_w2, mix_norm, w_in, conv_a, conv_b,
              ln_b_gain, ln_b_bias, pool_map, pool_scale, q_norm, k_norm, w_branch, w_gate,
              w_out, ffn2_norm, ffn2_w1, ffn2_w3, ffn2_w2):
    b, s, d = x.shape
    for l in range(DEPTH):
        x = x + 0.5 * swiglu(rms_norm(x, ffn1_norm[l]), ffn1_w1[l], ffn1_w3[l], ffn1_w2[l])

        h = rms_norm(x, mix_norm[l])
        p = (h @ w_in[l]).reshape(b, s, N_IN_SLOTS, BRANCH_WIDTH)
        a_x, a_b, a_c, b_val, b_gate, c_in, q, k, v = [p[:, :, i] for i in range(N_IN_SLOTS)]

        y_a = a_b * causal_depthwise_conv(a_c * a_x, conv_a[l])

        y_b = jax.nn.silu(layer_norm(
            causal_depthwise_conv(b_val * jax.nn.sigmoid(b_gate), conv_b[l]),
            ln_b_gain[l], ln_b_bias[l]))

        pooled = multiscale_pool(c_in).reshape(b, s, len(POOL_WINDOWS), POOL_GROUP)
        y_c = jnp.einsum("bsgc,gce->bsge", pooled, pool_map[l]).reshape(b, s, BRANCH_WIDTH)
        y_c = y_c * pool_scale[l]

        qh = rms_norm(q.reshape(b, s, SB_HEADS, SB_HEAD_DIM), q_norm[l])
        kh = rms_norm(k.reshape(b, s, SB_HEADS, SB_HEAD_DIM), k_norm[l])
        vh = v.reshape(b, s, SB_HEADS, SB_HEAD_DIM)
        y_d = stick_breaking_attention(qh, kh, vh)

        branches = jnp.stack([y_a, y_b, y_c, y_d], axis=2)
        up = jnp.einsum("bsnc,ncd->bsnd", branches, w_branch[l])
        gates = jax.nn.sigmoid(h @ w_gate[l]).reshape(b, s, N_BRANCH, d)
        x = x + jnp.sum(gates * up, axis=2) @ w_out[l]

        x = x + 0.5 * swiglu(rms_norm(x, ffn2_norm[l]), ffn2_w1[l], ffn2_w3[l], ffn2_w2[l])
    return x
```

```python
import contextlib
import numpy as np
import ml_dtypes
import concourse.bass as bass
import concourse.mybir as mybir
from concourse.bass_utils import run_bass_kernel_spmd

F32 = mybir.dt.float32
BF16 = mybir.dt.bfloat16
AF = mybir.ActivationFunctionType
ALU = mybir.AluOpType
NPBF = ml_dtypes.bfloat16

NCORES = 8
D = 2048
S = 8192
T = S // NCORES
FF = 5632
W = 512
EPS = 1e-6
COMPUTE = ("pe", "act", "dve", "pool")


class Prog:
    def __init__(self, nc):
        self.nc = nc
        self.ops = []
        self.last_w = {}
        self.rd = {}

    def op(self, eng, fn, reads=(), writes=(), chan=None):
        deps = set()
        for k in reads:
            w = self.last_w.get(k)
            if w is not None:
                deps.add(w)
        for k in writes:
            w = self.last_w.get(k)
            if w is not None:
                deps.add(w)
            r = self.rd.get(k)
            if r:
                deps.update(r[0].values())
                deps.update(r[1])
        i = len(self.ops)
        self.ops.append((eng, fn, deps, chan))
        for k in reads:
            r = self.rd.setdefault(k, ({}, []))
            if chan is None:
                r[0][eng] = i
            else:
                r[1].append(i)
        for k in writes:
            self.last_w[k] = i
            self.rd[k] = ({}, [])
        return i

    def emit(self):
        nc, ops = self.nc, self.ops
        n = len(ops)
        sig = [False] * n
        for (_, _, deps, _) in ops:
            for d in deps:
                sig[d] = True
        cnt = {e: 0 for e in COMPUTE}
        val = [0] * n
        ccnt = {}
        per_eng = {e: [] for e in COMPUTE + ("sp",)}
        for i, (eng, fn, deps, chan) in enumerate(ops):
            per_eng[eng].append(i)
            if chan is not None:
                ccnt[chan] = ccnt.get(chan, 0) + 16
                val[i] = ccnt[chan]
            elif sig[i]:
                cnt[eng] += 1
                val[i] = cnt[eng]
        with contextlib.ExitStack() as es:
            sems = {e: es.enter_context(nc.semaphore("s_" + e)) for e in COMPUTE}
            csems = {}
            for j, c in enumerate(ccnt):
                csems[c] = es.enter_context(nc.semaphore("c%d" % j))
            block = es.enter_context(nc.Block())

            def make(name):
                def body(e):
                    waited = {}
                    for i in per_eng[name]:
                        eng, fn, deps, chan = ops[i]
                        need = {}
                        for d in deps:
                            de, _, _, dch = ops[d]
                            if dch is not None:
                                key, s = ("c", dch), csems[dch]
                            else:
                                if de == name and name == "pe":
                                    continue
                                key, s = ("e", de), sems[de]
                            if need.get(key, (0, None))[0] < val[d]:
                                need[key] = (val[d], s)
                        for key, (v, s) in need.items():
                            if waited.get(key, 0) < v:
                                e.wait_ge(s, v)
                                waited[key] = v
                        ins = fn(e)
                        if chan is not None:
                            ins.then_inc(csems[chan], 16)
                        elif sig[i]:
                            ins.then_inc(sems[eng], 1)
                    if name == "sp":
                        for c, tot in ccnt.items():
                            e.wait_ge(csems[c], tot)
                return body

            block.tensor(make("pe"))
            block.scalar(make("act"))
            block.vector(make("dve"))
            block.gpsimd(make("pool"))
            block.sync(make("sp"))


class Ctx:
    pass


def alloc_common(nc, es, P, xn_bufs=2):
    c = Ctx()
    c.nc, c.P = nc, P
    sb = lambda name, shape, dt: es.enter_context(nc.sbuf_tensor("sb_" + name, shape, dt))
    c.acc = sb("acc", [128, 8, D], F32)
    c.hT = sb("hT", [128, 16, T], BF16)
    c.wb = sb("wb", [128, 12, 2048], BF16)
    c.aT = sb("aT", [128, 2, 2, T], BF16)
    c.sg = sb("sg", [128, 2, T], F32)
    c.gb = sb("gb", [128, D], F32)
    c.xn = sb("xn", [128, xn_bufs, D], BF16)
    c.xn_bufs = xn_bufs
    c.ss = sb("ss", [128, 16], F32)
    c.rstd = sb("rstd", [128, 16], F32)
    c.ident = sb("ident", [128, 128], BF16)
    c.epsb = sb("epsb", [128, 2], F32)
    P.op("pool", lambda e: e.memset(c.epsb[:, 0:1], EPS), writes=["epsb"])
    P.op("pool", lambda e: e.memset(c.epsb[:, 1:2], 64.0 * EPS), writes=["epsb"])
    c.ps = es.enter_context(nc.psum_tensor("ps", [128, 7, 512], F32))
    c.pT = es.enter_context(nc.psum_tensor("pT", [128, 8, 128], BF16))
    c.dbank = 0
    return c


def load_const(c, tile_ap, key, src_ap):
    c.P.op("sp", lambda e: e.dma_start(out=tile_ap, in_=src_ap), writes=[key], chan=key)


def wload(c, slot, src_ap, shape3=None):
    dst = c.wb[:, slot, :]
    if shape3 is not None:
        dst = dst.rearrange("p (k c) -> p k c", k=shape3[0])
    c.P.op("pool", lambda e: e.dma_start(out=dst, in_=src_ap), writes=[("wb", slot)], chan=("wb", slot))


def wcols(w_ap, c0, kh):
    return w_ap[kh * 1024:(kh + 1) * 1024, c0:c0 + 256].rearrange("(k p) c -> p k c", p=128)


def rmsnorm_to_hT(c, gain_key):
    P = c.P
    for tt in range(8):
        P.op("act", lambda e, tt=tt: e.activation(out=c.sg[:, :, :].rearrange("p a t -> p (a t)"), in_=c.acc[:, tt, :], func=AF.Square,
                                                  accum_out=c.ss[:, tt:tt + 1]),
             reads=[("acc", tt)], writes=[("sg", 0), ("sg", 1), "ss"])
    P.op("act", lambda e: e.activation(out=c.rstd[:, 0:8], in_=c.ss[:, 0:8], func=AF.Sqrt, scale=1.0 / D, bias=c.epsb[:, 0:1]),
         reads=["ss", "epsb"], writes=["rstd"])
    P.op("dve", lambda e: e.reciprocal(out=c.rstd[:, 0:8], in_=c.rstd[:, 0:8]), reads=["rstd"], writes=["rstd"])
    for tt in range(8):
        b = tt % c.xn_bufs
        P.op("dve", lambda e, tt=tt, b=b: e.scalar_tensor_tensor(out=c.xn[:, b, :], in0=c.acc[:, tt, :], scalar=c.rstd[:, tt:tt + 1],
                                                                 in1=c.gb[:, :], op0=ALU.mult, op1=ALU.mult),
             reads=[("acc", tt), "rstd", gain_key], writes=[("xn", b)])
        for half in range(2):
            for kk in range(8):
                k = half * 8 + kk
                P.op("pe", lambda e, k=k, kk=kk, b=b: e.transpose(out=c.pT[:, kk, :], in_=c.xn[:, b, k * 128:(k + 1) * 128],
                                                                  identity=c.ident[:, :]),
                     reads=[("xn", b), "ident"], writes=["pT"])
            eng = "act" if half == 0 else "dve"
            if eng == "act":
                fn = lambda e, tt=tt, half=half: e.copy(out=c.hT[:, half * 8:(half + 1) * 8, tt * 128:(tt + 1) * 128], in_=c.pT[:, :, :])
            else:
                fn = lambda e, tt=tt, half=half: e.tensor_copy(out=c.hT[:, half * 8:(half + 1) * 8, tt * 128:(tt + 1) * 128], in_=c.pT[:, :, :])
            P.op(eng, fn, reads=["pT"], writes=[("hT", k, tt // 4) for k in range(half * 8, half * 8 + 8)])


def down_tiles(c, a_fn, wslots, tiles, nj=2):
    P = c.P
    for (tt8, dc) in tiles:
        bank = 4 + c.dbank % 3
        c.dbank += 1
        for j in range(nj):
            ap, key = a_fn(j, tt8)
            P.op("pe", lambda e, ap=ap, j=j, dc=dc, bank=bank: e.matmul(c.ps[:, bank, :], lhsT=ap,
                                                                         rhs=c.wb[:, wslots[j], dc * 512:(dc + 1) * 512],
                                                                         start=(j == 0), stop=(j == nj - 1)),
                 reads=[key, ("wb", wslots[j])], writes=[("ps", bank)])
        P.op("dve", lambda e, tt8=tt8, dc=dc, bank=bank: e.tensor_tensor(out=c.acc[:, tt8, dc * 512:(dc + 1) * 512], in0=c.ps[:, bank, :],
                                                                          in1=c.acc[:, tt8, dc * 512:(dc + 1) * 512], op=ALU.add),
             reads=[("ps", bank), ("acc", tt8)], writes=[("acc", tt8)])


def fm_matmul(c, slots, cc, banks):
    P = c.P
    for k in range(16):
        for tt in range(2):
            P.op("pe", lambda e, k=k, tt=tt: e.matmul(c.ps[:, banks[tt], :],
                                                       lhsT=c.wb[:, slots[k // 8], (k % 8) * 256 + cc * 128:(k % 8) * 256 + cc * 128 + 128],
                                                       rhs=c.hT[:, k, tt * 512:(tt + 1) * 512], start=(k == 0), stop=(k == 15)),
                 reads=[("wb", slots[k // 8]), ("hT", k, tt)], writes=[("ps", banks[tt])])


def ffn(c, gain_key, w1, w3, w2, F):
    P = c.P
    rmsnorm_to_hT(c, gain_key)
    NG = F // 256
    ALLT = [(t8, dc) for t8 in range(8) for dc in range(4)]

    def req13(g):
        p = (g % 2) * 6
        for kh in range(2):
            wload(c, p + kh, wcols(w1, g * 256, kh), (8, 256))
            wload(c, p + 2 + kh, wcols(w3, g * 256, kh), (8, 256))

    def req2(g):
        p = (g % 2) * 6
        for j in range(2):
            wload(c, p + 4 + j, w2[g * 256 + j * 128:g * 256 + (j + 1) * 128, :])

    req13(0)
    req2(0)
    for g in range(NG + 1):
        p = (g % 2) * 6
        pending = []
        if g >= 1:
            pg = g - 1
            pp = (pg % 2) * 6
            a_fn = lambda j, t8, pg=pg: (c.aT[:, pg % 2, j, t8 * 128:(t8 + 1) * 128], ("aT", pg % 2, j))
            pending = [(a_fn, (pp + 4, pp + 5), ALLT[i * 8:(i + 1) * 8]) for i in range(4)]
        if g < NG:
            if g + 1 < NG:
                req13(g + 1)
            for j in range(2):
                fm_matmul(c, (p, p + 1), j, (0, 1))
                P.op("act", lambda e, j=j: e.activation(out=c.sg[:, j, :], in_=c.ps[:, 0:2, :].rearrange("p a t -> p (a t)"), func=AF.Silu),
                     reads=[("ps", 0), ("ps", 1)], writes=[("sg", j)])
                if pending:
                    down_tiles(c, *pending.pop(0))
                fm_matmul(c, (p + 2, p + 3), j, (2, 3))
                P.op("dve", lambda e, j=j, g=g: e.scalar_tensor_tensor(out=c.aT[:, g % 2, j, :], in0=c.ps[:, 2:4, :].rearrange("p a t -> p (a t)"),
                                                                        scalar=0.5, in1=c.sg[:, j, :], op0=ALU.mult, op1=ALU.mult),
                     reads=[("ps", 2), ("ps", 3), ("sg", j)], writes=[("aT", g % 2, j)])
                if pending:
                    down_tiles(c, *pending.pop(0))
        while pending:
            down_tiles(c, *pending.pop(0))
        if g + 1 < NG:
            req2(g + 1)


def dram_in(nc, name, shape, dt=F32):
    return nc.dram_tensor(name, list(shape), dt, kind="ExternalInput").ap()


def dram_out(nc, name, shape, dt=F32):
    return nc.dram_tensor(name, list(shape), dt, kind="ExternalOutput").ap()


def build_A(stage=9):
    nc = bass.Bass("TRN2", target_bir_lowering=False)
    x = dram_in(nc, "x", [T, D])
    g1 = dram_in(nc, "g1", [128, D])
    gm = dram_in(nc, "gm", [128, D])
    w1 = dram_in(nc, "w1", [D, FF])
    w3 = dram_in(nc, "w3", [D, FF])
    w2 = dram_in(nc, "w2", [FF, D])
    win = dram_in(nc, "win", [D, 9 * W])
    qkg = dram_in(nc, "qkg", [128, 2])
    ident = dram_in(nc, "ident", [128, 128], BF16)
    bones = dram_in(nc, "bones", [128, 128], BF16)
    x1 = dram_out(nc, "x1", [T, D])
    hTo = dram_out(nc, "hTo", [D, T], BF16)
    fo = dram_out(nc, "fo", [4, W, T])
    qkv = dram_out(nc, "qkv", [3, W, T], BF16)
    P = Prog(nc)
    with contextlib.ExitStack() as es:
        c = alloc_common(nc, es, P)
        sb = lambda name, shape, dt: es.enter_context(nc.sbuf_tensor("sb_" + name, shape, dt))
        ost = sb("ost", [128, 2, T], F32)
        obf = sb("obf", [128, 2, T], BF16)
        sqb = sb("sqb", [128, T], BF16)
        rs = sb("rs", [128, T], F32)
        qkg_t = sb("qkg_t", [128, 2], F32)
        bones_t = sb("bones_t", [128, 128], BF16)
        load_const(c, c.ident[:, :], "ident", ident)
        load_const(c, bones_t[:, :], "bones", bones)
        load_const(c, qkg_t[:, :], "qkg", qkg)
        load_const(c, c.gb[:, :], "gb", g1)
        for tt in range(8):
            P.op("sp", lambda e, tt=tt: e.dma_start(out=c.acc[:, tt, :], in_=x[tt * 128:(tt + 1) * 128, :]),
                 writes=[("acc", tt)], chan=("acc", tt))
        if stage == 1:
            rmsnorm_to_hT(c, "gb")
            for k in range(16):
                P.op("sp", lambda e, k=k: e.dma_start(out=hTo[k * 128:(k + 1) * 128, :], in_=c.hT[:, k, :]),
                     reads=[("hT", k, 0), ("hT", k, 1)], chan=("hTo", k % 4))
            P.op("sp", lambda e: e.dma_start(out=x1[0:128, 0:16], in_=c.rstd[:, :]), reads=["rstd"], chan="dbg")
            P.op("sp", lambda e: e.dma_start(out=x1[0:128, 16:32], in_=c.ss[:, :]), reads=["ss"], chan="dbg")
            P.emit()
            return nc
        ffn(c, "gb", w1, w3, w2, FF)
        for tt in range(8):
            P.op("sp", lambda e, tt=tt: e.dma_start(out=x1[tt * 128:(tt + 1) * 128, :], in_=c.acc[:, tt, :]),
                 reads=[("acc", tt)], chan=("x1o", tt))
        P.op("sp", lambda e: e.dma_start(out=c.gb[:, :], in_=gm), writes=["gb"], chan="gb")
        rmsnorm_to_hT(c, "gb")
        for k in range(16):
            P.op("sp", lambda e, k=k: e.dma_start(out=hTo[k * 128:(k + 1) * 128, :], in_=c.hT[:, k, :]),
                 reads=[("hT", k, 0), ("hT", k, 1)], chan=("hTo", k % 4))
        tmp = c.acc[:, :, :].rearrange("p a d -> p (a d)")
        tview = lambda i: tmp[:, i * 4096:(i + 1) * 4096].rearrange("p (c t) -> p c t", c=4)
        tkeys = lambda i: [("acc", 2 * i), ("acc", 2 * i + 1)]
        tA, tB, tQ = tview(0), tview(1), tview(2)
        nst = [0]

        def store(dst_ap, src_tile_ap, key, chname):
            P.op("sp", lambda e: e.dma_start(out=dst_ap, in_=src_tile_ap), reads=[key], chan=chname)

        for g in range(18):
            slot_pair = ((g % 2) * 6, (g % 2) * 6 + 1)
            for kh in range(2):
                wload(c, slot_pair[kh], wcols(win, g * 256, kh), (8, 256))
            s = g // 2
            for j in range(2):
                cc = (g % 2) * 2 + j
                banks = (0, 1) if j == 0 else (2, 3)
                fm_matmul(c, slot_pair, j, banks)
                psv = c.ps[:, banks[0]:banks[0] + 2, :].rearrange("p a t -> p (a t)")
                pk = [("ps", banks[0]), ("ps", banks[1])]
                ob = nst[0] % 2
                nst[0] += 1
                rows = slice(cc * 128, (cc + 1) * 128)
                if s == 0:
                    P.op("act", lambda e, psv=psv, cc=cc: e.copy(out=tA[:, cc, :], in_=psv), reads=pk, writes=tkeys(0))
                elif s == 3:
                    P.op("act", lambda e, psv=psv, cc=cc: e.copy(out=tB[:, cc, :], in_=psv), reads=pk, writes=tkeys(1))
                elif s in (1, 5):
                    P.op("act", lambda e, psv=psv, ob=ob: e.copy(out=ost[:, ob, :], in_=psv), reads=pk, writes=[("ost", ob)])
                    store(fo[1 if s == 1 else 3, rows, :], ost[:, ob, :], ("ost", ob), ("osto", ob))
                elif s == 2:
                    P.op("dve", lambda e, psv=psv, ob=ob, cc=cc: e.tensor_tensor(out=ost[:, ob, :], in0=psv, in1=tA[:, cc, :], op=ALU.mult),
                         reads=pk + tkeys(0), writes=[("ost", ob)])
                    store(fo[0, rows, :], ost[:, ob, :], ("ost", ob), ("osto", ob))
                elif s == 4:
                    P.op("act", lambda e, psv=psv, j=j: e.activation(out=c.sg[:, j, :], in_=psv, func=AF.Sigmoid), reads=pk, writes=[("sg", j)])
                    P.op("dve", lambda e, ob=ob, cc=cc, j=j: e.tensor_tensor(out=ost[:, ob, :], in0=c.sg[:, j, :], in1=tB[:, cc, :], op=ALU.mult),
                         reads=[("sg", j)] + tkeys(1), writes=[("ost", ob)])
                    store(fo[2, rows, :], ost[:, ob, :], ("ost", ob), ("osto", ob))
                elif s in (6, 7):
                    P.op("act", lambda e, psv=psv, cc=cc: e.copy(out=tQ[:, cc, :], in_=psv), reads=pk, writes=tkeys(2))
                    P.op("act", lambda e, psv=psv: e.activation(out=sqb[:, :], in_=psv, func=AF.Square), reads=pk, writes=["sqb"])
                    for tt in range(2):
                        P.op("pe", lambda e, tt=tt: e.matmul(c.ps[:, 4 + tt, :], lhsT=bones_t[:, :], rhs=sqb[:, tt * 512:(tt + 1) * 512],
                                                              start=True, stop=True),
                             reads=["bones", "sqb"], writes=[("ps", 4 + tt)])
                    sc, bcol = (1.0, 1) if s == 6 else (1.0 / 64.0, 0)
                    P.op("act", lambda e, sc=sc, bcol=bcol: e.activation(out=rs[:, :], in_=c.ps[:, 4:6, :].rearrange("p a t -> p (a t)"),
                                                                         func=AF.Sqrt, scale=sc, bias=c.epsb[:, bcol:bcol + 1]),
                         reads=[("ps", 4), ("ps", 5), "epsb"], writes=["rs"])
                    P.op("dve", lambda e: e.reciprocal(out=rs[:, :], in_=rs[:, :]), reads=["rs"], writes=["rs"])
                    P.op("dve", lambda e, cc=cc, ob=ob, s=s: e.scalar_tensor_tensor(out=obf[:, ob, :], in0=tQ[:, cc, :],
                                                                                   scalar=qkg_t[:, s - 6:s - 5], in1=rs[:, :],
                                                                                   op0=ALU.mult, op1=ALU.mult),
                         reads=tkeys(2) + ["rs", "qkg"], writes=[("obf", ob)])
                    store(qkv[s - 6, rows, :], obf[:, ob, :], ("obf", ob), ("obfo", ob))
                else:
                    P.op("act", lambda e, psv=psv, ob=ob: e.copy(out=obf[:, ob, :], in_=psv), reads=pk, writes=[("obf", ob)])
                    store(qkv[2, rows, :], obf[:, ob, :], ("obf", ob), ("obfo", ob))
        P.emit()
    return nc


def _consts():
    ident = np.eye(128, dtype=np.float32).astype(NPBF)
    bones = np.kron(np.eye(2, dtype=np.float32), np.ones((64, 64), np.float32)).astype(NPBF)
    return ident, bones


def bcast(v):
    return np.ascontiguousarray(np.broadcast_to(np.asarray(v, np.float32)[None, :], (128, v.shape[0])))


def run_A(xs, l, inp, nc=None):
    nc = nc or build_A()
    ident, bones = _consts()
    qkg = np.stack([np.tile(inp["q_norm"][l], 2), np.tile(inp["k_norm"][l], 2)], axis=1).astype(np.float32)
    common = dict(g1=bcast(inp["ffn1_norm"][l]), gm=bcast(inp["mix_norm"][l]), w1=inp["ffn1_w1"][l], w3=inp["ffn1_w3"][l],
                  w2=inp["ffn1_w2"][l], win=inp["w_in"][l], qkg=np.ascontiguousarray(qkg), ident=ident, bones=bones)
    maps = [dict(common, x=np.ascontiguousarray(xs[i])) for i in range(NCORES)]
    return run_bass_kernel_spmd(nc, maps, core_ids=list(range(NCORES))).results


def build_ATT(nq=16):
    nc = bass.Bass("TRN2", target_bir_lowering=False)
    qT = dram_in(nc, "qT", [64, S], BF16)
    kT = dram_in(nc, "kT", [64, S], BF16)
    v = dram_in(nc, "v", [128, 64, 64], BF16)
    negui = dram_in(nc, "negui", [128, 128], BF16)
    negone = dram_in(nc, "negone", [128, 128], BF16)
    ident = dram_in(nc, "ident", [128, 128], BF16)
    maska = dram_in(nc, "maska", [128, 4, 512], F32)
    maskb = dram_in(nc, "maskb", [128, 4, 512], BF16)
    yT = dram_out(nc, "yT", [64, S], BF16)
    P = Prog(nc)
    with contextlib.ExitStack() as es:
        sb = lambda name, shape, dt: es.enter_context(nc.sbuf_tensor("sb_" + name, shape, dt))
        q_t = sb("q", [64, S], BF16)
        k_t = sb("k", [64, S], BF16)
        v_t = sb("v", [128, 64, 64], BF16)
        ui_t = sb("ui", [128, 128], BF16)
        on_t = sb("on", [128, 128], BF16)
        id_t = sb("id", [128, 128], BF16)
        ma_t = sb("ma", [128, 4, 512], F32)
        mb_t = sb("mb", [128, 4, 512], BF16)
        NB = 3
        e1 = sb("e1", [128, 2, 2, 512], F32)
        sp = sb("sp", [128, NB, 2, 512], F32)
        spb = sb("spb", [128, NB, 2, 512], BF16)
        lsum = sb("lsum", [128, 512], F32)
        lb = sb("lb", [128, NB, 512], BF16)
        at = sb("at", [128, 2, 2, 512], BF16)
        yo = sb("yo", [64, 2, 512], BF16)
        ps = es.enter_context(nc.psum_tensor("ps", [128, 8, 512], F32))
        cmap = [("q", q_t, qT), ("k", k_t, kT), ("v", v_t, v), ("ui", ui_t, negui), ("on", on_t, negone), ("id", id_t, ident),
                ("ma", ma_t, maska), ("mb", mb_t, maskb)]
        for key, t_, src in cmap:
            if key in ("q", "k"):
                for h in range(4):
                    P.op("sp", lambda e, t_=t_, src=src, h=h: e.dma_start(out=t_[:, h * 2048:(h + 1) * 2048], in_=src[:, h * 2048:(h + 1) * 2048]),
                         writes=[(key, h)], chan=(key, h))
            else:
                P.op("sp", lambda e, t_=t_, src=src: e.dma_start(out=t_.ap(), in_=src), writes=[key], chan=key)
        pairs = []
        for qi in range(nq):
            npair = 2 * qi + 2
            for pi in range(npair):
                pairs.append((qi, pi, npair))

        def stage1(i):
            qi, pi, npair = pairs[i]
            pb = npair - 1 - pi
            zs = i % 2
            b3 = i % NB
            t0 = qi * 512
            for h in range(2):
                kb = 2 * pb + 1 - h
                P.op("pe", lambda e, kb=kb, h=h: e.matmul(ps[:, zs * 2 + h, :], lhsT=k_t[:, kb * 128:(kb + 1) * 128], rhs=q_t[:, t0:t0 + 512],
                                                         start=True, stop=True),
                     reads=[("k", kb // 16), ("q", qi // 4)], writes=[("ps", zs * 2 + h)])
            pz = [("ps", zs * 2), ("ps", zs * 2 + 1)]
            P.op("act", lambda e: e.activation(out=e1[:, zs, :, :].rearrange("p a t -> p (a t)"),
                                               in_=ps[:, zs * 2:zs * 2 + 2, :].rearrange("p a t -> p (a t)"), func=AF.Exp),
                 reads=pz, writes=[("e1", zs)])
            P.op("act", lambda e: e.activation(out=sp[:, b3, :, :].rearrange("p a t -> p (a t)"),
                                               in_=e1[:, zs, :, :].rearrange("p a t -> p (a t)"), func=AF.Ln, bias=1.0),
                 reads=[("e1", zs)], writes=[("sp", b3)])
            if pi < 2:
                r0 = 3 - 2 * pi
                P.op("dve", lambda e: e.tensor_tensor(out=sp[:, b3, :, :], in0=sp[:, b3, :, :], in1=ma_t[:, 3 - r0:5 - r0, :],
                                                      op=ALU.mult),
                     reads=[("sp", b3), "ma"], writes=[("sp", b3)])
            P.op("pool", lambda e: e.tensor_copy(out=spb[:, b3, :, :], in_=sp[:, b3, :, :]), reads=[("sp", b3)], writes=[("spb", b3)])
            if pi + 1 < npair:
                if pi == 0:
                    P.op("dve", lambda e: e.tensor_tensor(out=lsum[:, :], in0=sp[:, b3, 0, :], in1=sp[:, b3, 1, :], op=ALU.add),
                         reads=[("sp", b3)], writes=["lsum"])
                else:
                    P.op("dve", lambda e: e.tensor_tensor(out=lsum[:, :], in0=lsum[:, :], in1=sp[:, b3, 0, :], op=ALU.add),
                         reads=[("sp", b3), "lsum"], writes=["lsum"])
                    P.op("dve", lambda e: e.tensor_tensor(out=lsum[:, :], in0=lsum[:, :], in1=sp[:, b3, 1, :], op=ALU.add),
                         reads=[("sp", b3), "lsum"], writes=["lsum"])
                nb3 = (i + 1) % NB
                P.op("pool", lambda e: e.tensor_copy(out=lb[:, nb3, :], in_=lsum[:, :]), reads=["lsum"], writes=[("lb", nb3)])

        def stage2(i):
            qi, pi, npair = pairs[i]
            pb = npair - 1 - pi
            b3 = i % NB
            t0 = qi * 512
            a2 = i % 2
            for h in range(2):
                kb = 2 * pb + 1 - h
                bank = 4 + h
                mm = [(k_t[:, kb * 128:(kb + 1) * 128], q_t[:, t0:t0 + 512], [("k", kb // 16), ("q", qi // 4)]),
                      (ui_t[:, :], spb[:, b3, h, :], ["ui", ("spb", b3)])]
                if h == 1:
                    mm.append((on_t[:, :], spb[:, b3, 0, :], ["on", ("spb", b3)]))
                if pi > 0:
                    mm.append((on_t[:, :], lb[:, b3, :], ["on", ("lb", b3)]))
                if pi < 2:
                    r = 3 - 2 * pi - h
                    mm.append((id_t[:, :], mb_t[:, r, :], ["id", "mb"]))
                for j, (l_, r_, rk) in enumerate(mm):
                    P.op("pe", lambda e, l_=l_, r_=r_, j=j, n=len(mm), bank=bank: e.matmul(ps[:, bank, :], lhsT=l_, rhs=r_, start=(j == 0),
                                                                                         stop=(j == n - 1)),
                         reads=rk, writes=[("ps", bank)])
            P.op("act", lambda e: e.activation(out=at[:, a2, :, :].rearrange("p a t -> p (a t)"),
                                               in_=ps[:, 4:6, :].rearrange("p a t -> p (a t)"), func=AF.Exp),
                 reads=[("ps", 4), ("ps", 5)], writes=[("at", a2)])
            ob = 6 + qi % 2
            for h in range(2):
                kb = 2 * pb + 1 - h
                first = (pi == 0 and h == 0)
                last = (pi == npair - 1 and h == 1)
                P.op("pe", lambda e, kb=kb, h=h, first=first, last=last: e.matmul(ps[0:64, ob, :], lhsT=v_t[:, kb, :], rhs=at[:, a2, h, :],
                                                                                 start=first, stop=last),
                     reads=["v", ("at", a2)], writes=[("ps", ob)])
            if pi == npair - 1:
                P.op("dve", lambda e: e.tensor_copy(out=yo[:, qi % 2, :], in_=ps[0:64, ob, :]), reads=[("ps", ob)], writes=[("yo", qi % 2)])
                P.op("sp", lambda e: e.dma_start(out=yT[:, t0:t0 + 512], in_=yo[:, qi % 2, :]), reads=[("yo", qi % 2)], chan=("yo", qi % 2))

        n = len(pairs)
        for i in range(n + 1):
            if i < n:
                stage1(i)
            if i >= 1:
                stage2(i - 1)
        P.emit()
    return nc


def att_consts():
    j = np.arange(128)
    negui = -(j[:, None] >= j[None, :]).astype(np.float32)
    negone = -np.ones((128, 128), np.float32)
    tl = np.arange(512)
    ma = np.stack([(tl[None, :] > (r * 128 + j[:, None])).astype(np.float32) for r in range(4)], axis=1)
    mb = (1.0 - ma) * -30000.0
    return dict(negui=negui.astype(NPBF), negone=negone.astype(NPBF), ident=np.eye(128, dtype=np.float32).astype(NPBF),
                maska=np.ascontiguousarray(ma[:, ::-1, :]), maskb=mb.astype(NPBF))


def run_ATT(qT_all, kT_all, vT_all, nc=None):
    nc = nc or build_ATT()
    cst = att_consts()
    maps = []
    for h in range(NCORES):
        rows = slice(h * 64, (h + 1) * 64)
        vh = np.ascontiguousarray(vT_all[rows, :].T.reshape(64, 128, 64).transpose(1, 0, 2))
        maps.append(dict(cst, qT=np.ascontiguousarray(qT_all[rows]), kT=np.ascontiguousarray(kT_all[rows]), v=vh))
    res = run_bass_kernel_spmd(nc, maps, core_ids=list(range(NCORES))).results
    return np.concatenate([np.asarray(r["yT"]) for r in res], axis=0)


class WStream:
    def __init__(self, c, items, R=12):
        self.c, self.items, self.R = c, items, R
        self.next = 0
        self.released = set()
        self._try()

    def _try(self):
        while self.next < len(self.items) and (self.next < self.R or (self.next - self.R) in self.released):
            src, shape3 = self.items[self.next]
            wload(self.c, self.next % self.R, src, shape3)
            self.next += 1

    def slot(self, n):
        assert n < self.next, (n, self.next)
        return n % self.R

    def release(self, n):
        self.released.add(n)
        self._try()


def build_C():
    nc = bass.Bass("TRN2", target_bir_lowering=False)
    x1 = dram_in(nc, "x1", [T, D])
    hTi = dram_in(nc, "hTi", [D, T], BF16)
    uH = dram_in(nc, "uH", [W, 2 + T])
    ab = dram_in(nc, "ab", [W, T])
    gH = dram_in(nc, "gH", [W, 30 + T])
    cH = dram_in(nc, "cH", [W, 16 + T])
    ydT = dram_in(nc, "ydT", [W, T], BF16)
    cwa = dram_in(nc, "cwa", [128, 4, 3])
    cwb = dram_in(nc, "cwb", [128, 4, 31])
    lnp = dram_in(nc, "lnp", [128, 3, 4])
    invc = dram_in(nc, "invc", [128, 4, 16])
    pmap = dram_in(nc, "pmap", [4, 128, 128])
    wbr = dram_in(nc, "wbr", [4, W, D])
    wg = dram_in(nc, "wg", [D, 4 * D])
    wo = dram_in(nc, "wo", [D, D])
    g2 = dram_in(nc, "g2", [128, D])
    w1 = dram_in(nc, "w1", [D, FF])
    w3 = dram_in(nc, "w3", [D, FF])
    w2 = dram_in(nc, "w2", [FF, D])
    ident = dram_in(nc, "ident", [128, 128], BF16)
    xo = dram_out(nc, "xo", [T, D])
    P = Prog(nc)
    with contextlib.ExitStack() as es:
        c = alloc_common(nc, es, P, xn_bufs=1)
        sb = lambda name, shape, dt: es.enter_context(nc.sbuf_tensor("sb_" + name, shape, dt))
        yT = sb("yT", [128, 4, 4, T], BF16)
        cwa_t = sb("cwa", [128, 4, 3], F32)
        cwb_t = sb("cwb", [128, 4, 31], F32)
        lnp_t = sb("lnp", [128, 3, 4], F32)
        invc_t = sb("invc", [128, 4, 16], F32)
        t16 = sb("t16", [128, 16], F32)
        onesf = sb("onesf", [128, 128], F32)
        load_const(c, c.ident[:, :], "ident", ident)
        load_const(c, cwa_t[:, :, :], "cwa", cwa)
        load_const(c, cwb_t[:, :, :], "cwb", cwb)
        load_const(c, lnp_t[:, :, :], "lnp", lnp)
        load_const(c, invc_t[:, :, :], "invc", invc)
        P.op("pool", lambda e: e.memset(onesf[:, :], 1.0 / W), writes=["onesf"])
        for k in range(16):
            P.op("sp", lambda e, k=k: e.dma_start(out=c.hT[:, k, :], in_=hTi[k * 128:(k + 1) * 128, :]),
                 writes=[("hT", k, 0), ("hT", k, 1)], chan=("hTi", k % 4))
        for ch in range(4):
            P.op("sp", lambda e, ch=ch: e.dma_start(out=yT[:, 3, ch, :], in_=ydT[ch * 128:(ch + 1) * 128, :]),
                 writes=[("yT", 3, ch)], chan=("ydT", ch))
        P.op("pool", lambda e: e.dma_start(out=c.wb[:, 11, 0:512].rearrange("p (g e) -> p g e", g=4), in_=pmap.rearrange("g c e -> c g e")),
             writes=[("wb", 11)], chan=("wb", 11))
        pm = lambda g: c.wb[:, 11, g * 128:(g + 1) * 128]

        accf = c.acc[:, :, :].rearrange("p a d -> p (a d)")

        def tv(off, n):
            return accf[:, off:off + n], [("acc", i) for i in range(off // D, (off + n - 1) // D + 1)]

        cbuf, cbk = tv(0, 4096)
        cbv = cbuf.rearrange("p (c t) -> p c t", c=4)
        gin = [tv(4096, 1056), tv(5152, 1056)]
        uin, uk = tv(6208, 1056)
        abin, abk = tv(7264, 1024)
        oa, oak = tv(8288, 1024)
        cin, cik = tv(9312, 1056)
        s_a, sak = tv(10368, 1056)
        s_b, sbk = tv(11424, 1056)
        pl, plk = tv(12480, 1024)
        sq, sqk = tv(13504, 1024)
        rsb, rsk = tv(14528, 1024)
        plb = c.xn[:, 0, 0:T]

        for ch in range(4):
            rows = slice(ch * 128, (ch + 1) * 128)
            P.op("sp", lambda e, rows=rows: e.dma_start(out=uin[:, 0:2 + T], in_=uH[rows, :]), writes=uk, chan="uin")
            P.op("sp", lambda e, rows=rows: e.dma_start(out=abin, in_=ab[rows, :]), writes=abk, chan="abin")
            P.op("dve", lambda e, ch=ch: e.tensor_scalar(out=oa, in0=uin[:, 0:T], scalar1=cwa_t[:, ch, 0:1], scalar2=None, op0=ALU.mult),
                 reads=uk + ["cwa"], writes=oak)
            for k in (1, 2):
                P.op("dve", lambda e, ch=ch, k=k: e.scalar_tensor_tensor(out=oa, in0=uin[:, k:k + T], scalar=cwa_t[:, ch, k:k + 1], in1=oa,
                                                                         op0=ALU.mult, op1=ALU.add),
                     reads=uk + oak + ["cwa"], writes=oak)
            P.op("dve", lambda e, ch=ch: e.tensor_tensor(out=yT[:, 0, ch, :], in0=oa, in1=abin, op=ALU.mult),
                 reads=oak + abk, writes=[("yT", 0, ch)])
        L = 16 + T
        for g in range(4):
            rows = slice(g * 128, (g + 1) * 128)
            P.op("sp", lambda e, rows=rows: e.dma_start(out=cin[:, 0:L], in_=cH[rows, :]), writes=cik, chan="cin")
            steps = [(s_a, sak, cin, cik, 1), (s_b, sbk, s_a, sak, 2), (s_a, sak, s_b, sbk, 4), (s_b, sbk, s_a, sak, 8)]
            for (dst, dk, src, sk, sh) in steps[:g + 1]:
                lo = 2 * sh - 1
                P.op("dve", lambda e, dst=dst, src=src, sh=sh, lo=lo: e.tensor_tensor(out=dst[:, lo:L], in0=src[:, lo:L], in1=src[:, lo - sh:L - sh],
                                                                                      op=ALU.add),
                     reads=sk, writes=dk)
            win, wk = (s_a, sak) if g % 2 == 0 else (s_b, sbk)
            wdt = float(2 ** (g + 1))
            P.op("dve", lambda e, win=win, wdt=wdt: e.scalar_tensor_tensor(out=pl, in0=win[:, 16:L], scalar=1.0 / wdt, in1=cin[:, 16:L],
                                                                           op0=ALU.mult, op1=ALU.subtract),
                 reads=wk + cik, writes=plk)
            P.op("dve", lambda e, win=win, g=g: e.tensor_tensor(out=t16[:, :], in0=win[:, 16:32], in1=invc_t[:, g, :], op=ALU.mult),
                 reads=wk + ["invc"], writes=["t16"])
            P.op("dve", lambda e: e.tensor_tensor(out=pl[:, 0:16], in0=t16[:, :], in1=cin[:, 16:32], op=ALU.subtract),
                 reads=["t16"] + cik + plk, writes=plk)
            P.op("act", lambda e: e.copy(out=plb, in_=pl), reads=plk, writes=[("xn", 0)])
            for tt in range(2):
                P.op("pe", lambda e, g=g, tt=tt: e.matmul(c.ps[:, tt, :], lhsT=pm(g), rhs=plb[:, tt * 512:(tt + 1) * 512], start=True, stop=True),
                     reads=[("wb", 11), ("xn", 0)], writes=[("ps", tt)])
            P.op("dve", lambda e, g=g: e.tensor_scalar(out=yT[:, 2, g, :], in0=c.ps[:, 0:2, :].rearrange("p a t -> p (a t)"),
                                                       scalar1=lnp_t[:, 2, g:g + 1], scalar2=None, op0=ALU.mult),
                 reads=[("ps", 0), ("ps", 1), "lnp"], writes=[("yT", 2, g)])
        for ch in range(4):
            rows = slice(ch * 128, (ch + 1) * 128)
            gi, gk = gin[ch % 2]
            P.op("sp", lambda e, rows=rows, gi=gi: e.dma_start(out=gi[:, 0:30 + T], in_=gH[rows, :]), writes=gk, chan=("gin", ch % 2))
            eng = "dve"
            ck = [("acc", ch // 2)]
            P.op(eng, lambda e, ch=ch, gi=gi: e.tensor_scalar(out=cbv[:, ch, :], in0=gi[:, 0:T], scalar1=cwb_t[:, ch, 0:1], scalar2=None, op0=ALU.mult),
                 reads=gk + ["cwb"], writes=ck)
            for k in range(1, 31):
                P.op(eng, lambda e, ch=ch, gi=gi, k=k: e.scalar_tensor_tensor(out=cbv[:, ch, :], in0=gi[:, k:k + T], scalar=cwb_t[:, ch, k:k + 1],
                                                                               in1=cbv[:, ch, :], op0=ALU.mult, op1=ALU.add),
                     reads=gk + ck + ["cwb"], writes=ck)
        for tt in range(2):
            for ch in range(4):
                P.op("pe", lambda e, tt=tt, ch=ch: e.matmul(c.ps[:, 4 + tt, :], lhsT=onesf[:, :], rhs=cbv[:, ch, tt * 512:(tt + 1) * 512],
                                                            start=(ch == 0), stop=(ch == 3)),
                     reads=["onesf"] + cbk, writes=[("ps", 4 + tt)])
        for ch in range(4):
            P.op("dve", lambda e, ch=ch: e.tensor_tensor(out=cbv[:, ch, :], in0=cbv[:, ch, :], in1=c.ps[:, 4:6, :].rearrange("p a t -> p (a t)"),
                                                         op=ALU.subtract),
                 reads=[("ps", 4), ("ps", 5)] + cbk, writes=cbk)
        for ch in range(4):
            P.op("act", lambda e, ch=ch: e.activation(out=sq, in_=cbv[:, ch, :], func=AF.Square), reads=cbk, writes=sqk)
            for tt in range(2):
                P.op("pe", lambda e, tt=tt, ch=ch: e.matmul(c.ps[:, tt, :], lhsT=onesf[:, :], rhs=sq[:, tt * 512:(tt + 1) * 512],
                                                            start=(ch == 0), stop=(ch == 3)),
                     reads=["onesf"] + sqk, writes=[("ps", tt)])
        P.op("act", lambda e: e.activation(out=rsb, in_=c.ps[:, 0:2, :].rearrange("p a t -> p (a t)"), func=AF.Sqrt, bias=c.epsb[:, 0:1]),
             reads=[("ps", 0), ("ps", 1), "epsb"], writes=rsk)
        P.op("dve", lambda e: e.reciprocal(out=rsb, in_=rsb), reads=rsk, writes=rsk)
        for ch in range(4):
            P.op("dve", lambda e, ch=ch: e.tensor_tensor(out=cbv[:, ch, :], in0=cbv[:, ch, :], in1=rsb, op=ALU.mult), reads=cbk + rsk, writes=cbk)
            P.op("act", lambda e, ch=ch: e.activation(out=yT[:, 1, ch, :], in_=cbv[:, ch, :], func=AF.Silu, scale=lnp_t[:, 0, ch:ch + 1],
                                                      bias=lnp_t[:, 1, ch:ch + 1]),
                 reads=cbk + ["lnp"], writes=[("yT", 1, ch)])
        for tt in range(8):
            P.op("sp", lambda e, tt=tt: e.dma_start(out=c.acc[:, tt, :], in_=x1[tt * 128:(tt + 1) * 128, :]),
                 writes=[("acc", tt)], chan=("acc", tt))
        items = []
        for dc in range(16):
            for i in range(4):
                items.append((wg[:, i * D + dc * 128:i * D + (dc + 1) * 128].rearrange("(k p) c -> p k c", p=128), (16, 128)))
            items.append((wbr[:, :, dc * 128:(dc + 1) * 128].rearrange("i (k p) c -> p (i k) c", p=128), (16, 128)))
        ws = WStream(c, items, R=8)
        mf = c.gb[:, 0:T]
        ptmp = c.gb[:, T:2 * T]
        ALLT = [(t8, dq) for t8 in range(8) for dq in range(4)]
        pending = []
        for dc in range(16):
            j, cc = dc // 2, dc % 2
            base = dc * 5
            wload(c, 8 + (j % 2) * 2 + cc, wo[dc * 128:(dc + 1) * 128, :])
            for i in range(4):
                sl = ws.slot(base + i)
                for k in range(16):
                    for tt in range(2):
                        P.op("pe", lambda e, sl=sl, k=k, tt=tt: e.matmul(c.ps[:, tt, :], lhsT=c.wb[:, sl, k * 128:(k + 1) * 128],
                                                                         rhs=c.hT[:, k, tt * 512:(tt + 1) * 512], start=(k == 0), stop=(k == 15)),
                             reads=[("wb", sl), ("hT", k, tt)], writes=[("ps", tt)])
                ws.release(base + i)
                sgb = i % 2
                P.op("act", lambda e, sgb=sgb: e.activation(out=c.sg[:, sgb, :], in_=c.ps[:, 0:2, :].rearrange("p a t -> p (a t)"), func=AF.Sigmoid),
                     reads=[("ps", 0), ("ps", 1)], writes=[("sg", sgb)])
                slb = ws.slot(base + 4)
                for k in range(4):
                    for tt in range(2):
                        P.op("pe", lambda e, slb=slb, i=i, k=k, tt=tt: e.matmul(c.ps[:, 2 + tt, :], lhsT=c.wb[:, slb, (i * 4 + k) * 128:(i * 4 + k + 1) * 128],
                                                                                rhs=yT[:, i, k, tt * 512:(tt + 1) * 512], start=(k == 0), stop=(k == 3)),
                             reads=[("wb", slb), ("yT", i, k)], writes=[("ps", 2 + tt)])
                if i == 3:
                    ws.release(base + 4)
                dst, dk = (mf, "mf") if i == 0 else (ptmp, "ptmp")
                P.op("dve", lambda e, dst=dst, sgb=sgb: e.tensor_tensor(out=dst, in0=c.ps[:, 2:4, :].rearrange("p a t -> p (a t)"), in1=c.sg[:, sgb, :],
                                                                        op=ALU.mult),
                     reads=[("ps", 2), ("ps", 3), ("sg", sgb)], writes=[dk])
                if i > 0:
                    P.op("pool", lambda e: e.tensor_tensor(out=mf, in0=mf, in1=ptmp, op=ALU.add), reads=["mf", "ptmp"], writes=["mf"])
                if pending and i % 2 == 1:
                    down_tiles(c, *pending.pop(0))
            P.op("act", lambda e, j=j, cc=cc: e.copy(out=c.aT[:, j % 2, cc, :], in_=mf), reads=["mf"], writes=[("aT", j % 2, cc)])
            if cc == 1:
                while pending:
                    down_tiles(c, *pending.pop(0))
                wsl = (8 + (j % 2) * 2, 8 + (j % 2) * 2 + 1)
                a_fn = lambda jj, t8, j=j: (c.aT[:, j % 2, jj, t8 * 128:(t8 + 1) * 128], ("aT", j % 2, jj))
                pending = [(a_fn, wsl, ALLT[q * 8:(q + 1) * 8]) for q in range(4)]
            if cc == 0 and j > 0:
                while pending:
                    down_tiles(c, *pending.pop(0))
        while pending:
            down_tiles(c, *pending.pop(0))
        P.op("sp", lambda e: e.dma_start(out=c.gb[:, :], in_=g2), reads=["mf", "ptmp"], writes=["gb", "mf", "ptmp"], chan="gb")
        ffn(c, "gb", w1, w3, w2, FF)
        for tt in range(8):
            P.op("sp", lambda e, tt=tt: e.dma_start(out=xo[tt * 128:(tt + 1) * 128, :], in_=c.acc[:, tt, :]),
                 reads=[("acc", tt)], chan=("xo", tt))
        P.emit()
    return nc


_NC = {}


def _prog(name):
    if name not in _NC:
        _NC[name] = {"A": build_A, "ATT": build_ATT, "C": build_C}[name]()
    return _NC[name]


def halo(parts, n):
    out = []
    for i in range(NCORES):
        prev = parts[i - 1][:, T - n:] if i > 0 else np.zeros((parts[0].shape[0], n), parts[0].dtype)
        out.append(np.ascontiguousarray(np.concatenate([prev, parts[i]], axis=1)))
    return out


def pp(v):
    return np.ascontiguousarray(np.asarray(v, np.float32).reshape(4, 128).T)


def run_C(resA, ydT_all, l, inp):
    ident, _ = _consts()
    uH = halo([np.asarray(r["fo"][0]) for r in resA], 2)
    gH = halo([np.asarray(r["fo"][2]) for r in resA], 30)
    cH = halo([np.asarray(r["fo"][3]) for r in resA], 16)
    cwa = np.ascontiguousarray(inp["conv_a"][l].reshape(3, 4, 128).transpose(2, 1, 0))
    cwb = np.ascontiguousarray(inp["conv_b"][l].reshape(31, 4, 128).transpose(2, 1, 0))
    lnp = np.ascontiguousarray(np.stack([pp(inp["ln_b_gain"][l]), pp(inp["ln_b_bias"][l]), pp(inp["pool_scale"][l])], axis=1))
    common = dict(cwa=cwa, cwb=cwb, lnp=lnp, pmap=inp["pool_map"][l], wbr=inp["w_branch"][l], wg=inp["w_gate"][l], wo=inp["w_out"][l],
                  g2=bcast(inp["ffn2_norm"][l]), w1=inp["ffn2_w1"][l], w3=inp["ffn2_w3"][l], w2=inp["ffn2_w2"][l], ident=ident)
    maps = []
    for i in range(NCORES):
        pos = np.arange(16) + i * T
        invc = np.stack([1.0 / np.minimum(pos + 1, 2 ** (g + 1)) for g in range(4)]).astype(np.float32)
        maps.append(dict(common, x1=np.asarray(resA[i]["x1"]), hTi=np.asarray(resA[i]["hTo"]), uH=uH[i], ab=np.asarray(resA[i]["fo"][1]),
                         gH=gH[i], cH=cH[i], ydT=np.ascontiguousarray(ydT_all[:, i * T:(i + 1) * T]),
                         invc=np.ascontiguousarray(np.broadcast_to(invc[None], (128, 4, 16)))))
    return run_bass_kernel_spmd(_prog("C"), maps, core_ids=list(range(NCORES))).results


def kernel(**inp):
    inp = {k: np.asarray(v) for k, v in inp.items()}
    x = inp["x"][0]
    xs = [x[i * T:(i + 1) * T] for i in range(NCORES)]
    for l in range(2):
        resA = run_A(xs, l, inp, _prog("A"))
        qkv = [np.concatenate([np.asarray(r["qkv"][j]) for r in resA], axis=1) for j in range(3)]
        ydT = run_ATT(qkv[0], qkv[1], qkv[2], _prog("ATT"))
        resC = run_C(resA, ydT, l, inp)
        xs = [np.asarray(r["xo"]) for r in resC]
    return np.concatenate(xs, axis=0)[None].astype(np.float32)
```

```python
import contextlib
import numpy as np
import ml_dtypes
import concourse.bass as bass
import concourse.mybir as mybir
from concourse.bass_utils import run_bass_kernel_spmd

F32 = mybir.dt.float32
BF16 = mybir.dt.bfloat16
AF = mybir.ActivationFunctionType
ALU = mybir.AluOpType
NPBF = ml_dtypes.bfloat16

NCORES = 8
D = 2048
S = 8192
T = S // NCORES
FF = 5632
W = 512
EPS = 1e-6
COMPUTE = ("pe", "act", "dve", "pool")


class Prog:
    def __init__(self, nc):
        self.nc = nc
        self.ops = []
        self.last_w = {}
        self.rd = {}

    def op(self, eng, fn, reads=(), writes=(), chan=None):
        deps = set()
        for k in reads:
            w = self.last_w.get(k)
            if w is not None:
                deps.add(w)
        for k in writes:
            w = self.last_w.get(k)
            if w is not None:
                deps.add(w)
            r = self.rd.get(k)
            if r:
                deps.update(r[0].values())
                deps.update(r[1])
        i = len(self.ops)
        self.ops.append((eng, fn, deps, chan))
        for k in reads:
            r = self.rd.setdefault(k, ({}, []))
            if chan is None:
                r[0][eng] = i
            else:
                r[1].append(i)
        for k in writes:
            self.last_w[k] = i
            self.rd[k] = ({}, [])
        return i

    def emit(self):
        nc, ops = self.nc, self.ops
        n = len(ops)
        sig = [False] * n
        for (_, _, deps, _) in ops:
            for d in deps:
                sig[d] = True
        cnt = {e: 0 for e in COMPUTE}
        val = [0] * n
        ccnt = {}
        per_eng = {e: [] for e in COMPUTE + ("sp",)}
        for i, (eng, fn, deps, chan) in enumerate(ops):
            per_eng[eng].append(i)
            if chan is not None:
                ccnt[chan] = ccnt.get(chan, 0) + 16
                val[i] = ccnt[chan]
            elif sig[i]:
                cnt[eng] += 1
                val[i] = cnt[eng]
        with contextlib.ExitStack() as es:
            sems = {e: es.enter_context(nc.semaphore("s_" + e)) for e in COMPUTE}
            csems = {}
            for j, c in enumerate(ccnt):
                csems[c] = es.enter_context(nc.semaphore("c%d" % j))
            block = es.enter_context(nc.Block())

            def make(name):
                def body(e):
                    waited = {}
                    for i in per_eng[name]:
                        eng, fn, deps, chan = ops[i]
                        need = {}
                        for d in deps:
                            de, _, _, dch = ops[d]
                            if dch is not None:
                                key, s = ("c", dch), csems[dch]
                            else:
                                if de == name and name == "pe":
                                    continue
                                key, s = ("e", de), sems[de]
                            if need.get(key, (0, None))[0] < val[d]:
                                need[key] = (val[d], s)
                        for key, (v, s) in need.items():
                            if waited.get(key, 0) < v:
                                e.wait_ge(s, v)
                                waited[key] = v
                        ins = fn(e)
                        if chan is not None:
                            ins.then_inc(csems[chan], 16)
                        elif sig[i]:
                            ins.then_inc(sems[eng], 1)
                    if name == "sp":
                        for c, tot in ccnt.items():
                            e.wait_ge(csems[c], tot)
                return body

            block.tensor(make("pe"))
            block.scalar(make("act"))
            block.vector(make("dve"))
            block.gpsimd(make("pool"))
            block.sync(make("sp"))


class Ctx:
    pass


def alloc_common(nc, es, P, xn_bufs=2):
    c = Ctx()
    c.nc, c.P = nc, P
    sb = lambda name, shape, dt: es.enter_context(nc.sbuf_tensor("sb_" + name, shape, dt))
    c.acc = sb("acc", [128, 8, D], F32)
    c.hT = sb("hT", [128, 16, T], BF16)
    c.wb = sb("wb", [128, 12, 2048], BF16)
    c.aT = sb("aT", [128, 2, 2, T], BF16)
    c.sg = sb("sg", [128, 2, T], F32)
    c.gb = sb("gb", [128, D], F32)
    c.xn = sb("xn", [128, xn_bufs, D], BF16)
    c.xn_bufs = xn_bufs
    c.ss = sb("ss", [128, 16], F32)
    c.rstd = sb("rstd", [128, 16], F32)
    c.ident = sb("ident", [128, 128], BF16)
    c.epsb = sb("epsb", [128, 2], F32)
    P.op("pool", lambda e: e.memset(c.epsb[:, 0:1], EPS), writes=["epsb"])
    P.op("pool", lambda e: e.memset(c.epsb[:, 1:2], 64.0 * EPS), writes=["epsb"])
    c.ps = es.enter_context(nc.psum_tensor("ps", [128, 7, 512], F32))
    c.pT = es.enter_context(nc.psum_tensor("pT", [128, 8, 128], BF16))
    c.dbank = 0
    return c


def load_const(c, tile_ap, key, src_ap):
    c.P.op("sp", lambda e: e.dma_start(out=tile_ap, in_=src_ap), writes=[key], chan=key)


def wload(c, slot, src_ap, shape3=None):
    dst = c.wb[:, slot, :]
    if shape3 is not None:
        dst = dst.rearrange("p (k c) -> p k c", k=shape3[0])
    c.P.op("pool", lambda e: e.dma_start(out=dst, in_=src_ap), writes=[("wb", slot)], chan=("wb", slot))


def wcols(w_ap, c0, kh):
    return w_ap[kh * 1024:(kh + 1) * 1024, c0:c0 + 256].rearrange("(k p) c -> p k c", p=128)


def rmsnorm_to_hT(c, gain_key):
    P = c.P
    for tt in range(8):
        P.op("act", lambda e, tt=tt: e.activation(out=c.sg[:, :, :].rearrange("p a t -> p (a t)"), in_=c.acc[:, tt, :], func=AF.Square,
                                                  accum_out=c.ss[:, tt:tt + 1]),
             reads=[("acc", tt)], writes=[("sg", 0), ("sg", 1), "ss"])
    P.op("act", lambda e: e.activation(out=c.rstd[:, 0:8], in_=c.ss[:, 0:8], func=AF.Sqrt, scale=1.0 / D, bias=c.epsb[:, 0:1]),
         reads=["ss", "epsb"], writes=["rstd"])
    P.op("dve", lambda e: e.reciprocal(out=c.rstd[:, 0:8], in_=c.rstd[:, 0:8]), reads=["rstd"], writes=["rstd"])
    for tt in range(8):
        b = tt % c.xn_bufs
        P.op("dve", lambda e, tt=tt, b=b: e.scalar_tensor_tensor(out=c.xn[:, b, :], in0=c.acc[:, tt, :], scalar=c.rstd[:, tt:tt + 1],
                                                                 in1=c.gb[:, :], op0=ALU.mult, op1=ALU.mult),
             reads=[("acc", tt), "rstd", gain_key], writes=[("xn", b)])
        for half in range(2):
            for kk in range(8):
                k = half * 8 + kk
                P.op("pe", lambda e, k=k, kk=kk, b=b: e.transpose(out=c.pT[:, kk, :], in_=c.xn[:, b, k * 128:(k + 1) * 128],
                                                                  identity=c.ident[:, :]),
                     reads=[("xn", b), "ident"], writes=["pT"])
            eng = "act" if half == 0 else "dve"
            if eng == "act":
                fn = lambda e, tt=tt, half=half: e.copy(out=c.hT[:, half * 8:(half + 1) * 8, tt * 128:(tt + 1) * 128], in_=c.pT[:, :, :])
            else:
                fn = lambda e, tt=tt, half=half: e.tensor_copy(out=c.hT[:, half * 8:(half + 1) * 8, tt * 128:(tt + 1) * 128], in_=c.pT[:, :, :])
            P.op(eng, fn, reads=["pT"], writes=[("hT", k, tt // 4) for k in range(half * 8, half * 8 + 8)])


def down_tiles(c, a_fn, wslots, tiles, nj=2):
    P = c.P
    for (tt8, dc) in tiles:
        bank = 4 + c.dbank % 3
        c.dbank += 1
        for j in range(nj):
            ap, key = a_fn(j, tt8)
            P.op("pe", lambda e, ap=ap, j=j, dc=dc, bank=bank: e.matmul(c.ps[:, bank, :], lhsT=ap,
                                                                         rhs=c.wb[:, wslots[j], dc * 512:(dc + 1) * 512],
                                                                         start=(j == 0), stop=(j == nj - 1)),
                 reads=[key, ("wb", wslots[j])], writes=[("ps", bank)])
        P.op("dve", lambda e, tt8=tt8, dc=dc, bank=bank: e.tensor_tensor(out=c.acc[:, tt8, dc * 512:(dc + 1) * 512], in0=c.ps[:, bank, :],
                                                                          in1=c.acc[:, tt8, dc * 512:(dc + 1) * 512], op=ALU.add),
             reads=[("ps", bank), ("acc", tt8)], writes=[("acc", tt8)])


def fm_matmul(c, slots, cc, banks):
    P = c.P
    for k in range(16):
        for tt in range(2):
            P.op("pe", lambda e, k=k, tt=tt: e.matmul(c.ps[:, banks[tt], :],
                                                       lhsT=c.wb[:, slots[k // 8], (k % 8) * 256 + cc * 128:(k % 8) * 256 + cc * 128 + 128],
                                                       rhs=c.hT[:, k, tt * 512:(tt + 1) * 512], start=(k == 0), stop=(k == 15)),
                 reads=[("wb", slots[k // 8]), ("hT", k, tt)], writes=[("ps", banks[tt])])


def ffn(c, gain_key, w1, w3, w2, F):
    P = c.P
    rmsnorm_to_hT(c, gain_key)
    NG = F // 256
    ALLT = [(t8, dc) for t8 in range(8) for dc in range(4)]

    def req13(g):
        p = (g % 2) * 6
        for kh in range(2):
            wload(c, p + kh, wcols(w1, g * 256, kh), (8, 256))
            wload(c, p + 2 + kh, wcols(w3, g * 256, kh), (8, 256))

    def req2(g):
        p = (g % 2) * 6
        for j in range(2):
            wload(c, p + 4 + j, w2[g * 256 + j * 128:g * 256 + (j + 1) * 128, :])

    req13(0)
    req2(0)
    for g in range(NG + 1):
        p = (g % 2) * 6
        pending = []
        if g >= 1:
            pg = g - 1
            pp = (pg % 2) * 6
            a_fn = lambda j, t8, pg=pg: (c.aT[:, pg % 2, j, t8 * 128:(t8 + 1) * 128], ("aT", pg % 2, j))
            pending = [(a_fn, (pp + 4, pp + 5), ALLT[i * 8:(i + 1) * 8]) for i in range(4)]
        if g < NG:
            if g + 1 < NG:
                req13(g + 1)
            for j in range(2):
                fm_matmul(c, (p, p + 1), j, (0, 1))
                P.op("act", lambda e, j=j: e.activation(out=c.sg[:, j, :], in_=c.ps[:, 0:2, :].rearrange("p a t -> p (a t)"), func=AF.Silu),
                     reads=[("ps", 0), ("ps", 1)], writes=[("sg", j)])
                if pending:
                    down_tiles(c, *pending.pop(0))
                fm_matmul(c, (p + 2, p + 3), j, (2, 3))
                P.op("dve", lambda e, j=j, g=g: e.scalar_tensor_tensor(out=c.aT[:, g % 2, j, :], in0=c.ps[:, 2:4, :].rearrange("p a t -> p (a t)"),
                                                                        scalar=0.5, in1=c.sg[:, j, :], op0=ALU.mult, op1=ALU.mult),
                     reads=[("ps", 2), ("ps", 3), ("sg", j)], writes=[("aT", g % 2, j)])
                if pending:
                    down_tiles(c, *pending.pop(0))
        while pending:
            down_tiles(c, *pending.pop(0))
        if g + 1 < NG:
            req2(g + 1)


def dram_in(nc, name, shape, dt=F32):
    return nc.dram_tensor(name, list(shape), dt, kind="ExternalInput").ap()


def dram_out(nc, name, shape, dt=F32):
    return nc.dram_tensor(name, list(shape), dt, kind="ExternalOutput").ap()


def build_A(stage=9):
    nc = bass.Bass("TRN2", target_bir_lowering=False)
    x = dram_in(nc, "x", [T, D])
    g1 = dram_in(nc, "g1", [128, D])
    gm = dram_in(nc, "gm", [128, D])
    w1 = dram_in(nc, "w1", [D, FF])
    w3 = dram_in(nc, "w3", [D, FF])
    w2 = dram_in(nc, "w2", [FF, D])
    win = dram_in(nc, "win", [D, 9 * W])
    qkg = dram_in(nc, "qkg", [128, 2])
    ident = dram_in(nc, "ident", [128, 128], BF16)
    bones = dram_in(nc, "bones", [128, 128], BF16)
    x1 = dram_out(nc, "x1", [T, D])
    hTo = dram_out(nc, "hTo", [D, T], BF16)
    fo = dram_out(nc, "fo", [4, W, T])
    qkv = dram_out(nc, "qkv", [3, W, T], BF16)
    P = Prog(nc)
    with contextlib.ExitStack() as es:
        c = alloc_common(nc, es, P)
        sb = lambda name, shape, dt: es.enter_context(nc.sbuf_tensor("sb_" + name, shape, dt))
        ost = sb("ost", [128, 2, T], F32)
        obf = sb("obf", [128, 2, T], BF16)
        sqb = sb("sqb", [128, T], BF16)
        rs = sb("rs", [128, T], F32)
        qkg_t = sb("qkg_t", [128, 2], F32)
        bones_t = sb("bones_t", [128, 128], BF16)
        load_const(c, c.ident[:, :], "ident", ident)
        load_const(c, bones_t[:, :], "bones", bones)
        load_const(c, qkg_t[:, :], "qkg", qkg)
        load_const(c, c.gb[:, :], "gb", g1)
        for tt in range(8):
            P.op("sp", lambda e, tt=tt: e.dma_start(out=c.acc[:, tt, :], in_=x[tt * 128:(tt + 1) * 128, :]),
                 writes=[("acc", tt)], chan=("acc", tt))
        if stage == 1:
            rmsnorm_to_hT(c, "gb")
            for k in range(16):
                P.op("sp", lambda e, k=k: e.dma_start(out=hTo[k * 128:(k + 1) * 128, :], in_=c.hT[:, k, :]),
                     reads=[("hT", k, 0), ("hT", k, 1)], chan=("hTo", k % 4))
            P.op("sp", lambda e: e.dma_start(out=x1[0:128, 0:16], in_=c.rstd[:, :]), reads=["rstd"], chan="dbg")
            P.op("sp", lambda e: e.dma_start(out=x1[0:128, 16:32], in_=c.ss[:, :]), reads=["ss"], chan="dbg")
            P.emit()
            return nc
        ffn(c, "gb", w1, w3, w2, FF)
        for tt in range(8):
            P.op("sp", lambda e, tt=tt: e.dma_start(out=x1[tt * 128:(tt + 1) * 128, :], in_=c.acc[:, tt, :]),
                 reads=[("acc", tt)], chan=("x1o", tt))
        P.op("sp", lambda e: e.dma_start(out=c.gb[:, :], in_=gm), writes=["gb"], chan="gb")
        rmsnorm_to_hT(c, "gb")
        for k in range(16):
            P.op("sp", lambda e, k=k: e.dma_start(out=hTo[k * 128:(k + 1) * 128, :], in_=c.hT[:, k, :]),
                 reads=[("hT", k, 0), ("hT", k, 1)], chan=("hTo", k % 4))
        tmp = c.acc[:, :, :].rearrange("p a d -> p (a d)")
        tview = lambda i: tmp[:, i * 4096:(i + 1) * 4096].rearrange("p (c t) -> p c t", c=4)
        tkeys = lambda i: [("acc", 2 * i), ("acc", 2 * i + 1)]
        tA, tB, tQ = tview(0), tview(1), tview(2)
        nst = [0]

        def store(dst_ap, src_tile_ap, key, chname):
            P.op("sp", lambda e: e.dma_start(out=dst_ap, in_=src_tile_ap), reads=[key], chan=chname)

        for g in range(18):
            slot_pair = ((g % 2) * 6, (g % 2) * 6 + 1)
            for kh in range(2):
                wload(c, slot_pair[kh], wcols(win, g * 256, kh), (8, 256))
            s = g // 2
            for j in range(2):
                cc = (g % 2) * 2 + j
                banks = (0, 1) if j == 0 else (2, 3)
                fm_matmul(c, slot_pair, j, banks)
                psv = c.ps[:, banks[0]:banks[0] + 2, :].rearrange("p a t -> p (a t)")
                pk = [("ps", banks[0]), ("ps", banks[1])]
                ob = nst[0] % 2
                nst[0] += 1
                rows = slice(cc * 128, (cc + 1) * 128)
                if s == 0:
                    P.op("act", lambda e, psv=psv, cc=cc: e.copy(out=tA[:, cc, :], in_=psv), reads=pk, writes=tkeys(0))
                elif s == 3:
                    P.op("act", lambda e, psv=psv, cc=cc: e.copy(out=tB[:, cc, :], in_=psv), reads=pk, writes=tkeys(1))
                elif s in (1, 5):
                    P.op("act", lambda e, psv=psv, ob=ob: e.copy(out=ost[:, ob, :], in_=psv), reads=pk, writes=[("ost", ob)])
                    store(fo[1 if s == 1 else 3, rows, :], ost[:, ob, :], ("ost", ob), ("osto", ob))
                elif s == 2:
                    P.op("dve", lambda e, psv=psv, ob=ob, cc=cc: e.tensor_tensor(out=ost[:, ob, :], in0=psv, in1=tA[:, cc, :], op=ALU.mult),
                         reads=pk + tkeys(0), writes=[("ost", ob)])
                    store(fo[0, rows, :], ost[:, ob, :], ("ost", ob), ("osto", ob))
                elif s == 4:
                    P.op("act", lambda e, psv=psv, j=j: e.activation(out=c.sg[:, j, :], in_=psv, func=AF.Sigmoid), reads=pk, writes=[("sg", j)])
                    P.op("dve", lambda e, ob=ob, cc=cc, j=j: e.tensor_tensor(out=ost[:, ob, :], in0=c.sg[:, j, :], in1=tB[:, cc, :], op=ALU.mult),
                         reads=[("sg", j)] + tkeys(1), writes=[("ost", ob)])
                    store(fo[2, rows, :], ost[:, ob, :], ("ost", ob), ("osto", ob))
                elif s in (6, 7):
                    P.op("act", lambda e, psv=psv, cc=cc: e.copy(out=tQ[:, cc, :], in_=psv), reads=pk, writes=tkeys(2))
                    P.op("act", lambda e, psv=psv: e.activation(out=sqb[:, :], in_=psv, func=AF.Square), reads=pk, writes=["sqb"])
                    for tt in range(2):
                        P.op("pe", lambda e, tt=tt: e.matmul(c.ps[:, 4 + tt, :], lhsT=bones_t[:, :], rhs=sqb[:, tt * 512:(tt + 1) * 512],
                                                              start=True, stop=True),
                             reads=["bones", "sqb"], writes=[("ps", 4 + tt)])
                    sc, bcol = (1.0, 1) if s == 6 else (1.0 / 64.0, 0)
                    P.op("act", lambda e, sc=sc, bcol=bcol: e.activation(out=rs[:, :], in_=c.ps[:, 4:6, :].rearrange("p a t -> p (a t)"),
                                                                         func=AF.Sqrt, scale=sc, bias=c.epsb[:, bcol:bcol + 1]),
                         reads=[("ps", 4), ("ps", 5), "epsb"], writes=["rs"])
                    P.op("dve", lambda e: e.reciprocal(out=rs[:, :], in_=rs[:, :]), reads=["rs"], writes=["rs"])
                    P.op("dve", lambda e, cc=cc, ob=ob, s=s: e.scalar_tensor_tensor(out=obf[:, ob, :], in0=tQ[:, cc, :],
                                                                                   scalar=qkg_t[:, s - 6:s - 5], in1=rs[:, :],
                                                                                   op0=ALU.mult, op1=ALU.mult),
                         reads=tkeys(2) + ["rs", "qkg"], writes=[("obf", ob)])
                    store(qkv[s - 6, rows, :], obf[:, ob, :], ("obf", ob), ("obfo", ob))
                else:
                    P.op("act", lambda e, psv=psv, ob=ob: e.copy(out=obf[:, ob, :], in_=psv), reads=pk, writes=[("obf", ob)])
                    store(qkv[2, rows, :], obf[:, ob, :], ("obf", ob), ("obfo", ob))
        P.emit()
    return nc


def _consts():
    ident = np.eye(128, dtype=np.float32).astype(NPBF)
    bones = np.kron(np.eye(2, dtype=np.float32), np.ones((64, 64), np.float32)).astype(NPBF)
    return ident, bones


def bcast(v):
    return np.ascontiguousarray(np.broadcast_to(np.asarray(v, np.float32)[None, :], (128, v.shape[0])))


def run_A(xs, l, inp, nc=None):
    nc = nc or build_A()
    ident, bones = _consts()
    qkg = np.stack([np.tile(inp["q_norm"][l], 2), np.tile(inp["k_norm"][l], 2)], axis=1).astype(np.float32)
    common = dict(g1=bcast(inp["ffn1_norm"][l]), gm=bcast(inp["mix_norm"][l]), w1=inp["ffn1_w1"][l], w3=inp["ffn1_w3"][l],
                  w2=inp["ffn1_w2"][l], win=inp["w_in"][l], qkg=np.ascontiguousarray(qkg), ident=ident, bones=bones)
    maps = [dict(common, x=np.ascontiguousarray(xs[i])) for i in range(NCORES)]
    return run_bass_kernel_spmd(nc, maps, core_ids=list(range(NCORES))).results


def build_ATT(nq=16):
    nc = bass.Bass("TRN2", target_bir_lowering=False)
    qT = dram_in(nc, "qT", [64, S], BF16)
    kT = dram_in(nc, "kT", [64, S], BF16)
    v = dram_in(nc, "v", [128, 64, 64], BF16)
    negui = dram_in(nc, "negui", [128, 128], BF16)
    negone = dram_in(nc, "negone", [128, 128], BF16)
    ident = dram_in(nc, "ident", [128, 128], BF16)
    maska = dram_in(nc, "maska", [128, 4, 512], BF16)
    maskb = dram_in(nc, "maskb", [128, 4, 512], BF16)
    yT = dram_out(nc, "yT", [64, S], BF16)
    P = Prog(nc)
    with contextlib.ExitStack() as es:
        sb = lambda name, shape, dt: es.enter_context(nc.sbuf_tensor("sb_" + name, shape, dt))
        q_t = sb("q", [64, S], BF16)
        k_t = sb("k", [64, S], BF16)
        v_t = sb("v", [128, 64, 64], BF16)
        ui_t = sb("ui", [128, 128], BF16)
        on_t = sb("on", [128, 128], BF16)
        id_t = sb("id", [128, 128], BF16)
        mab_t = sb("mab", [128, 4, 512], BF16)
        mb_t = sb("mb", [128, 4, 512], BF16)
        NB = 3
        e1 = sb("e1", [128, 2, 2, 512], F32)
        spb = sb("spb", [128, NB, 2, 512], BF16)
        lsum = sb("lsum", [128, 512], F32)
        lb = sb("lb", [128, NB, 512], BF16)
        at = sb("at", [128, 2, 2, 512], BF16)
        yo = sb("yo", [64, 2, 512], BF16)
        ps = es.enter_context(nc.psum_tensor("ps", [128, 8, 512], F32))
        cmap = [("q", q_t, qT), ("k", k_t, kT), ("v", v_t, v), ("ui", ui_t, negui), ("on", on_t, negone), ("id", id_t, ident),
                ("mab", mab_t, maska), ("mb", mb_t, maskb)]
        for key, t_, src in cmap:
            if key in ("q", "k"):
                for h in range(4):
                    P.op("sp", lambda e, t_=t_, src=src, h=h: e.dma_start(out=t_[:, h * 2048:(h + 1) * 2048], in_=src[:, h * 2048:(h + 1) * 2048]),
                         writes=[(key, h)], chan=(key, h))
            else:
                P.op("sp", lambda e, t_=t_, src=src: e.dma_start(out=t_.ap(), in_=src), writes=[key], chan=key)
        pairs = []
        for qi in range(nq):
            npair = 2 * qi + 2
            for pi in range(npair):
                pairs.append((qi, pi, npair))

        def stage1(i):
            qi, pi, npair = pairs[i]
            pb = npair - 1 - pi
            zs = i % 2
            b3 = i % NB
            t0 = qi * 512
            for h in range(2):
                kb = 2 * pb + 1 - h
                P.op("pe", lambda e, kb=kb, h=h: e.matmul(ps[:, h, :], lhsT=k_t[:, kb * 128:(kb + 1) * 128], rhs=q_t[:, t0:t0 + 512],
                                                         start=True, stop=True),
                     reads=[("k", kb // 16), ("q", qi // 4)], writes=[("ps", h)])
            pz = [("ps", 0), ("ps", 1)]
            P.op("act", lambda e: e.activation(out=e1[:, zs, :, :].rearrange("p a t -> p (a t)"),
                                               in_=ps[:, 0:2, :].rearrange("p a t -> p (a t)"), func=AF.Exp),
                 reads=pz, writes=[("e1", zs)])
            P.op("act", lambda e: e.activation(out=spb[:, b3, :, :].rearrange("p a t -> p (a t)"),
                                               in_=e1[:, zs, :, :].rearrange("p a t -> p (a t)"), func=AF.Ln, bias=1.0),
                 reads=[("e1", zs)], writes=[("spb", b3)])
            if pi < 2:
                r0 = 3 - 2 * pi
                P.op("dve", lambda e: e.tensor_tensor(out=spb[:, b3, :, :], in0=spb[:, b3, :, :], in1=mab_t[:, 3 - r0:5 - r0, :], op=ALU.mult),
                     reads=[("spb", b3), "mab"], writes=[("spb", b3)])
            if pi + 1 < npair:
                if pi == 0:
                    P.op("dve", lambda e: e.tensor_tensor(out=lsum[:, :], in0=spb[:, b3, 0, :], in1=spb[:, b3, 1, :], op=ALU.add),
                         reads=[("spb", b3)], writes=["lsum"])
                else:
                    P.op("dve", lambda e: e.tensor_tensor(out=lsum[:, :], in0=lsum[:, :], in1=spb[:, b3, 0, :], op=ALU.add),
                         reads=[("spb", b3), "lsum"], writes=["lsum"])
                    P.op("dve", lambda e: e.tensor_tensor(out=lsum[:, :], in0=lsum[:, :], in1=spb[:, b3, 1, :], op=ALU.add),
                         reads=[("spb", b3), "lsum"], writes=["lsum"])
                nb3 = (i + 1) % NB
                P.op("dve", lambda e: e.tensor_copy(out=lb[:, nb3, :], in_=lsum[:, :]), reads=["lsum"], writes=[("lb", nb3)])

        def stage2(i):
            qi, pi, npair = pairs[i]
            pb = npair - 1 - pi
            b3 = i % NB
            t0 = qi * 512
            a2 = i % 2
            bb = 2 + 2 * (i % 2)
            for h in range(2):
                kb = 2 * pb + 1 - h
                bank = bb + h
                mm = [(k_t[:, kb * 128:(kb + 1) * 128], q_t[:, t0:t0 + 512], [("k", kb // 16), ("q", qi // 4)]),
                      (ui_t[:, :], spb[:, b3, h, :], ["ui", ("spb", b3)])]
                if h == 1:
                    mm.append((on_t[:, :], spb[:, b3, 0, :], ["on", ("spb", b3)]))
                if pi > 0:
                    mm.append((on_t[:, :], lb[:, b3, :], ["on", ("lb", b3)]))
                if pi < 2:
                    r = 3 - 2 * pi - h
                    mm.append((id_t[:, :], mb_t[:, r, :], ["id", "mb"]))
                for j, (l_, r_, rk) in enumerate(mm):
                    P.op("pe", lambda e, l_=l_, r_=r_, j=j, n=len(mm), bank=bank: e.matmul(ps[:, bank, :], lhsT=l_, rhs=r_, start=(j == 0),
                                                                                         stop=(j == n - 1)),
                         reads=rk, writes=[("ps", bank)])
            P.op("act", lambda e: e.activation(out=at[:, a2, :, :].rearrange("p a t -> p (a t)"),
                                               in_=ps[:, bb:bb + 2, :].rearrange("p a t -> p (a t)"), func=AF.Exp),
                 reads=[("ps", bb), ("ps", bb + 1)], writes=[("at", a2)])

        def stage3(i):
            qi, pi, npair = pairs[i]
            pb = npair - 1 - pi
            t0 = qi * 512
            a2 = i % 2
            ob = 6 + qi % 2
            for h in range(2):
                kb = 2 * pb + 1 - h
                first = (pi == 0 and h == 0)
                last = (pi == npair - 1 and h == 1)
                P.op("pe", lambda e, kb=kb, h=h, first=first, last=last: e.matmul(ps[0:64, ob, :], lhsT=v_t[:, kb, :], rhs=at[:, a2, h, :],
                                                                                 start=first, stop=last),
                     reads=["v", ("at", a2)], writes=[("ps", ob)])
            if pi == npair - 1:
                P.op("dve", lambda e: e.tensor_copy(out=yo[:, qi % 2, :], in_=ps[0:64, ob, :]), reads=[("ps", ob)], writes=[("yo", qi % 2)])
                P.op("sp", lambda e: e.dma_start(out=yT[:, t0:t0 + 512], in_=yo[:, qi % 2, :]), reads=[("yo", qi % 2)], chan=("yo", qi % 2))

        n = len(pairs)
        for i in range(n + 2):
            if i < n:
                stage1(i)
            if 1 <= i <= n:
                stage2(i - 1)
            if i >= 2:
                stage3(i - 2)
        P.emit()
    return nc


def att_consts():
    j = np.arange(128)
    negui = -(j[:, None] >= j[None, :]).astype(np.float32)
    negone = -np.ones((128, 128), np.float32)
    tl = np.arange(512)
    ma = np.stack([(tl[None, :] > (r * 128 + j[:, None])).astype(np.float32) for r in range(4)], axis=1)
    mb = (1.0 - ma) * -30000.0
    return dict(negui=negui.astype(NPBF), negone=negone.astype(NPBF), ident=np.eye(128, dtype=np.float32).astype(NPBF),
                maska=np.ascontiguousarray(ma[:, ::-1, :]).astype(NPBF), maskb=mb.astype(NPBF))


def run_ATT(qT_all, kT_all, vT_all, nc=None):
    nc = nc or build_ATT()
    cst = att_consts()
    maps = []
    for h in range(NCORES):
        rows = slice(h * 64, (h + 1) * 64)
        vh = np.ascontiguousarray(vT_all[rows, :].T.reshape(64, 128, 64).transpose(1, 0, 2))
        maps.append(dict(cst, qT=np.ascontiguousarray(qT_all[rows]), kT=np.ascontiguousarray(kT_all[rows]), v=vh))
    res = run_bass_kernel_spmd(nc, maps, core_ids=list(range(NCORES))).results
    return np.concatenate([np.asarray(r["yT"]) for r in res], axis=0)


class WStream:
    def __init__(self, c, items, R=12):
        self.c, self.items, self.R = c, items, R
        self.next = 0
        self.released = set()
        self._try()

    def _try(self):
        while self.next < len(self.items) and (self.next < self.R or (self.next - self.R) in self.released):
            src, shape3 = self.items[self.next]
            wload(self.c, self.next % self.R, src, shape3)
            self.next += 1

    def slot(self, n):
        assert n < self.next, (n, self.next)
        return n % self.R

    def release(self, n):
        self.released.add(n)
        self._try()


def build_C():
    nc = bass.Bass("TRN2", target_bir_lowering=False)
    x1 = dram_in(nc, "x1", [T, D])
    hTi = dram_in(nc, "hTi", [D, T], BF16)
    uH = dram_in(nc, "uH", [W, 2 + T])
    ab = dram_in(nc, "ab", [W, T])
    gH = dram_in(nc, "gH", [W, 30 + T])
    cH = dram_in(nc, "cH", [W, 16 + T])
    ydT = dram_in(nc, "ydT", [W, T], BF16)
    cwa = dram_in(nc, "cwa", [128, 4, 3])
    cwb = dram_in(nc, "cwb", [128, 4, 31])
    lnp = dram_in(nc, "lnp", [128, 3, 4])
    invc = dram_in(nc, "invc", [128, 4, 16])
    pmap = dram_in(nc, "pmap", [4, 128, 128])
    wbr = dram_in(nc, "wbr", [4, W, D])
    wg = dram_in(nc, "wg", [D, 4 * D])
    wo = dram_in(nc, "wo", [D, D])
    g2 = dram_in(nc, "g2", [128, D])
    w1 = dram_in(nc, "w1", [D, FF])
    w3 = dram_in(nc, "w3", [D, FF])
    w2 = dram_in(nc, "w2", [FF, D])
    ident = dram_in(nc, "ident", [128, 128], BF16)
    xo = dram_out(nc, "xo", [T, D])
    P = Prog(nc)
    with contextlib.ExitStack() as es:
        c = alloc_common(nc, es, P, xn_bufs=1)
        sb = lambda name, shape, dt: es.enter_context(nc.sbuf_tensor("sb_" + name, shape, dt))
        yT = sb("yT", [128, 4, 4, T], BF16)
        cwa_t = sb("cwa", [128, 4, 3], F32)
        cwb_t = sb("cwb", [128, 4, 31], F32)
        lnp_t = sb("lnp", [128, 3, 4], F32)
        invc_t = sb("invc", [128, 4, 16], F32)
        t16 = sb("t16", [128, 16], F32)
        onesf = sb("onesf", [128, 128], F32)
        load_const(c, c.ident[:, :], "ident", ident)
        load_const(c, cwa_t[:, :, :], "cwa", cwa)
        load_const(c, cwb_t[:, :, :], "cwb", cwb)
        load_const(c, lnp_t[:, :, :], "lnp", lnp)
        load_const(c, invc_t[:, :, :], "invc", invc)
        P.op("pool", lambda e: e.memset(onesf[:, :], 1.0 / W), writes=["onesf"])
        for k in range(16):
            P.op("sp", lambda e, k=k: e.dma_start(out=c.hT[:, k, :], in_=hTi[k * 128:(k + 1) * 128, :]),
                 writes=[("hT", k, 0), ("hT", k, 1)], chan=("hTi", k % 4))
        for ch in range(4):
            P.op("sp", lambda e, ch=ch: e.dma_start(out=yT[:, 3, ch, :], in_=ydT[ch * 128:(ch + 1) * 128, :]),
                 writes=[("yT", 3, ch)], chan=("ydT", ch))
        P.op("pool", lambda e: e.dma_start(out=c.wb[:, 11, 0:512].rearrange("p (g e) -> p g e", g=4), in_=pmap.rearrange("g c e -> c g e")),
             writes=[("wb", 11)], chan=("wb", 11))
        pm = lambda g: c.wb[:, 11, g * 128:(g + 1) * 128]

        accf = c.acc[:, :, :].rearrange("p a d -> p (a d)")

        def tv(off, n):
            return accf[:, off:off + n], [("acc", i) for i in range(off // D, (off + n - 1) // D + 1)]

        cbuf, cbk = tv(0, 4096)
        cbv = cbuf.rearrange("p (c t) -> p c t", c=4)
        gin = [tv(4096, 1056), tv(5152, 1056)]
        uin, uk = tv(6208, 1056)
        abin, abk = tv(7264, 1024)
        oa, oak = tv(8288, 1024)
        cin, cik = tv(9312, 1056)
        s_a, sak = tv(10368, 1056)
        s_b, sbk = tv(11424, 1056)
        pl, plk = tv(12480, 1024)
        sq, sqk = tv(13504, 1024)
        rsb, rsk = tv(14528, 1024)
        plb = c.xn[:, 0, 0:T]

        for ch in range(4):
            rows = slice(ch * 128, (ch + 1) * 128)
            P.op("sp", lambda e, rows=rows: e.dma_start(out=uin[:, 0:2 + T], in_=uH[rows, :]), writes=uk, chan="uin")
            P.op("sp", lambda e, rows=rows: e.dma_start(out=abin, in_=ab[rows, :]), writes=abk, chan="abin")
            P.op("dve", lambda e, ch=ch: e.tensor_scalar(out=oa, in0=uin[:, 0:T], scalar1=cwa_t[:, ch, 0:1], scalar2=None, op0=ALU.mult),
                 reads=uk + ["cwa"], writes=oak)
            for k in (1, 2):
                P.op("dve", lambda e, ch=ch, k=k: e.scalar_tensor_tensor(out=oa, in0=uin[:, k:k + T], scalar=cwa_t[:, ch, k:k + 1], in1=oa,
                                                                         op0=ALU.mult, op1=ALU.add),
                     reads=uk + oak + ["cwa"], writes=oak)
            P.op("dve", lambda e, ch=ch: e.tensor_tensor(out=yT[:, 0, ch, :], in0=oa, in1=abin, op=ALU.mult),
                 reads=oak + abk, writes=[("yT", 0, ch)])
        L = 16 + T
        for g in range(4):
            rows = slice(g * 128, (g + 1) * 128)
            P.op("sp", lambda e, rows=rows: e.dma_start(out=cin[:, 0:L], in_=cH[rows, :]), writes=cik, chan="cin")
            steps = [(s_a, sak, cin, cik, 1), (s_b, sbk, s_a, sak, 2), (s_a, sak, s_b, sbk, 4), (s_b, sbk, s_a, sak, 8)]
            for (dst, dk, src, sk, sh) in steps[:g + 1]:
                lo = 2 * sh - 1
                P.op("dve", lambda e, dst=dst, src=src, sh=sh, lo=lo: e.tensor_tensor(out=dst[:, lo:L], in0=src[:, lo:L], in1=src[:, lo - sh:L - sh],
                                                                                      op=ALU.add),
                     reads=sk, writes=dk)
            win, wk = (s_a, sak) if g % 2 == 0 else (s_b, sbk)
            wdt = float(2 ** (g + 1))
            P.op("dve", lambda e, win=win, wdt=wdt: e.scalar_tensor_tensor(out=pl, in0=win[:, 16:L], scalar=1.0 / wdt, in1=cin[:, 16:L],
                                                                           op0=ALU.mult, op1=ALU.subtract),
                 reads=wk + cik, writes=plk)
            P.op("dve", lambda e, win=win, g=g: e.tensor_tensor(out=t16[:, :], in0=win[:, 16:32], in1=invc_t[:, g, :], op=ALU.mult),
                 reads=wk + ["invc"], writes=["t16"])
            P.op("dve", lambda e: e.tensor_tensor(out=pl[:, 0:16], in0=t16[:, :], in1=cin[:, 16:32], op=ALU.subtract),
                 reads=["t16"] + cik + plk, writes=plk)
            P.op("act", lambda e: e.copy(out=plb, in_=pl), reads=plk, writes=[("xn", 0)])
            for tt in range(2):
                P.op("pe", lambda e, g=g, tt=tt: e.matmul(c.ps[:, tt, :], lhsT=pm(g), rhs=plb[:, tt * 512:(tt + 1) * 512], start=True, stop=True),
                     reads=[("wb", 11), ("xn", 0)], writes=[("ps", tt)])
            P.op("dve", lambda e, g=g: e.tensor_scalar(out=yT[:, 2, g, :], in0=c.ps[:, 0:2, :].rearrange("p a t -> p (a t)"),
                                                       scalar1=lnp_t[:, 2, g:g + 1], scalar2=None, op0=ALU.mult),
                 reads=[("ps", 0), ("ps", 1), "lnp"], writes=[("yT", 2, g)])
        for ch in range(4):
            rows = slice(ch * 128, (ch + 1) * 128)
            gi, gk = gin[ch % 2]
            P.op("sp", lambda e, rows=rows, gi=gi: e.dma_start(out=gi[:, 0:30 + T], in_=gH[rows, :]), writes=gk, chan=("gin", ch % 2))
            eng = "dve"
            ck = [("acc", ch // 2)]
            P.op(eng, lambda e, ch=ch, gi=gi: e.tensor_scalar(out=cbv[:, ch, :], in0=gi[:, 0:T], scalar1=cwb_t[:, ch, 0:1], scalar2=None, op0=ALU.mult),
                 reads=gk + ["cwb"], writes=ck)
            for k in range(1, 31):
                P.op(eng, lambda e, ch=ch, gi=gi, k=k: e.scalar_tensor_tensor(out=cbv[:, ch, :], in0=gi[:, k:k + T], scalar=cwb_t[:, ch, k:k + 1],
                                                                               in1=cbv[:, ch, :], op0=ALU.mult, op1=ALU.add),
                     reads=gk + ck + ["cwb"], writes=ck)
        for tt in range(2):
            for ch in range(4):
                P.op("pe", lambda e, tt=tt, ch=ch: e.matmul(c.ps[:, 4 + tt, :], lhsT=onesf[:, :], rhs=cbv[:, ch, tt * 512:(tt + 1) * 512],
                                                            start=(ch == 0), stop=(ch == 3)),
                     reads=["onesf"] + cbk, writes=[("ps", 4 + tt)])
        for ch in range(4):
            P.op("dve", lambda e, ch=ch: e.tensor_tensor(out=cbv[:, ch, :], in0=cbv[:, ch, :], in1=c.ps[:, 4:6, :].rearrange("p a t -> p (a t)"),
                                                         op=ALU.subtract),
                 reads=[("ps", 4), ("ps", 5)] + cbk, writes=cbk)
        for ch in range(4):
            P.op("act", lambda e, ch=ch: e.activation(out=sq, in_=cbv[:, ch, :], func=AF.Square), reads=cbk, writes=sqk)
            for tt in range(2):
                P.op("pe", lambda e, tt=tt, ch=ch: e.matmul(c.ps[:, tt, :], lhsT=onesf[:, :], rhs=sq[:, tt * 512:(tt + 1) * 512],
                                                            start=(ch == 0), stop=(ch == 3)),
                     reads=["onesf"] + sqk, writes=[("ps", tt)])
        P.op("act", lambda e: e.activation(out=rsb, in_=c.ps[:, 0:2, :].rearrange("p a t -> p (a t)"), func=AF.Sqrt, bias=c.epsb[:, 0:1]),
             reads=[("ps", 0), ("ps", 1), "epsb"], writes=rsk)
        P.op("dve", lambda e: e.reciprocal(out=rsb, in_=rsb), reads=rsk, writes=rsk)
        for ch in range(4):
            P.op("dve", lambda e, ch=ch: e.tensor_tensor(out=cbv[:, ch, :], in0=cbv[:, ch, :], in1=rsb, op=ALU.mult), reads=cbk + rsk, writes=cbk)
            P.op("act", lambda e, ch=ch: e.activation(out=yT[:, 1, ch, :], in_=cbv[:, ch, :], func=AF.Silu, scale=lnp_t[:, 0, ch:ch + 1],
                                                      bias=lnp_t[:, 1, ch:ch + 1]),
                 reads=cbk + ["lnp"], writes=[("yT", 1, ch)])
        for tt in range(8):
            P.op("sp", lambda e, tt=tt: e.dma_start(out=c.acc[:, tt, :], in_=x1[tt * 128:(tt + 1) * 128, :]),
                 writes=[("acc", tt)], chan=("acc", tt))
        items = []
        for dc in range(16):
            for i in range(4):
                items.append((wg[:, i * D + dc * 128:i * D + (dc + 1) * 128].rearrange("(k p) c -> p k c", p=128), (16, 128)))
            items.append((wbr[:, :, dc * 128:(dc + 1) * 128].rearrange("i (k p) c -> p (i k) c", p=128), (16, 128)))
        ws = WStream(c, items, R=8)
        mf = c.gb[:, 0:T]
        ptmp = c.gb[:, T:2 * T]
        ALLT = [(t8, dq) for t8 in range(8) for dq in range(4)]
        pending = []
        for dc in range(16):
            j, cc = dc // 2, dc % 2
            base = dc * 5
            wload(c, 8 + (j % 2) * 2 + cc, wo[dc * 128:(dc + 1) * 128, :])
            for i in range(4):
                sl = ws.slot(base + i)
                for k in range(16):
                    for tt in range(2):
                        P.op("pe", lambda e, sl=sl, k=k, tt=tt: e.matmul(c.ps[:, tt, :], lhsT=c.wb[:, sl, k * 128:(k + 1) * 128],
                                                                         rhs=c.hT[:, k, tt * 512:(tt + 1) * 512], start=(k == 0), stop=(k == 15)),
                             reads=[("wb", sl), ("hT", k, tt)], writes=[("ps", tt)])
                ws.release(base + i)
                sgb = i % 2
                P.op("act", lambda e, sgb=sgb: e.activation(out=c.sg[:, sgb, :], in_=c.ps[:, 0:2, :].rearrange("p a t -> p (a t)"), func=AF.Sigmoid),
                     reads=[("ps", 0), ("ps", 1)], writes=[("sg", sgb)])
                slb = ws.slot(base + 4)
                for k in range(4):
                    for tt in range(2):
                        P.op("pe", lambda e, slb=slb, i=i, k=k, tt=tt: e.matmul(c.ps[:, 2 + tt, :], lhsT=c.wb[:, slb, (i * 4 + k) * 128:(i * 4 + k + 1) * 128],
                                                                                rhs=yT[:, i, k, tt * 512:(tt + 1) * 512], start=(k == 0), stop=(k == 3)),
                             reads=[("wb", slb), ("yT", i, k)], writes=[("ps", 2 + tt)])
                if i == 3:
                    ws.release(base + 4)
                dst, dk = (mf, "mf") if i == 0 else (ptmp, "ptmp")
                P.op("dve", lambda e, dst=dst, sgb=sgb: e.tensor_tensor(out=dst, in0=c.ps[:, 2:4, :].rearrange("p a t -> p (a t)"), in1=c.sg[:, sgb, :],
                                                                        op=ALU.mult),
                     reads=[("ps", 2), ("ps", 3), ("sg", sgb)], writes=[dk])
                if i > 0:
                    P.op("pool", lambda e: e.tensor_tensor(out=mf, in0=mf, in1=ptmp, op=ALU.add), reads=["mf", "ptmp"], writes=["mf"])
                if pending and i % 2 == 1:
                    down_tiles(c, *pending.pop(0))
            P.op("act", lambda e, j=j, cc=cc: e.copy(out=c.aT[:, j % 2, cc, :], in_=mf), reads=["mf"], writes=[("aT", j % 2, cc)])
            if cc == 1:
                while pending:
                    down_tiles(c, *pending.pop(0))
                wsl = (8 + (j % 2) * 2, 8 + (j % 2) * 2 + 1)
                a_fn = lambda jj, t8, j=j: (c.aT[:, j % 2, jj, t8 * 128:(t8 + 1) * 128], ("aT", j % 2, jj))
                pending = [(a_fn, wsl, ALLT[q * 8:(q + 1) * 8]) for q in range(4)]
            if cc == 0 and j > 0:
                while pending:
                    down_tiles(c, *pending.pop(0))
        while pending:
            down_tiles(c, *pending.pop(0))
        P.op("sp", lambda e: e.dma_start(out=c.gb[:, :], in_=g2), reads=["mf", "ptmp"], writes=["gb", "mf", "ptmp"], chan="gb")
        ffn(c, "gb", w1, w3, w2, FF)
        for tt in range(8):
            P.op("sp", lambda e, tt=tt: e.dma_start(out=xo[tt * 128:(tt + 1) * 128, :], in_=c.acc[:, tt, :]),
                 reads=[("acc", tt)], chan=("xo", tt))
        P.emit()
    return nc


_NC = {}


def _prog(name):
    if name not in _NC:
        _NC[name] = {"A": build_A, "ATT": build_ATT, "C": build_C}[name]()
    return _NC[name]


def halo(parts, n):
    out = []
    for i in range(NCORES):
        prev = parts[i - 1][:, T - n:] if i > 0 else np.zeros((parts[0].shape[0], n), parts[0].dtype)
        out.append(np.ascontiguousarray(np.concatenate([prev, parts[i]], axis=1)))
    return out


def pp(v):
    return np.ascontiguousarray(np.asarray(v, np.float32).reshape(4, 128).T)


def run_C(resA, ydT_all, l, inp):
    ident, _ = _consts()
    uH = halo([np.asarray(r["fo"][0]) for r in resA], 2)
    gH = halo([np.asarray(r["fo"][2]) for r in resA], 30)
    cH = halo([np.asarray(r["fo"][3]) for r in resA], 16)
    cwa = np.ascontiguousarray(inp["conv_a"][l].reshape(3, 4, 128).transpose(2, 1, 0))
    cwb = np.ascontiguousarray(inp["conv_b"][l].reshape(31, 4, 128).transpose(2, 1, 0))
    lnp = np.ascontiguousarray(np.stack([pp(inp["ln_b_gain"][l]), pp(inp["ln_b_bias"][l]), pp(inp["pool_scale"][l])], axis=1))
    common = dict(cwa=cwa, cwb=cwb, lnp=lnp, pmap=inp["pool_map"][l], wbr=inp["w_branch"][l], wg=inp["w_gate"][l], wo=inp["w_out"][l],
                  g2=bcast(inp["ffn2_norm"][l]), w1=inp["ffn2_w1"][l], w3=inp["ffn2_w3"][l], w2=inp["ffn2_w2"][l], ident=ident)
    maps = []
    for i in range(NCORES):
        pos = np.arange(16) + i * T
        invc = np.stack([1.0 / np.minimum(pos + 1, 2 ** (g + 1)) for g in range(4)]).astype(np.float32)
        maps.append(dict(common, x1=np.asarray(resA[i]["x1"]), hTi=np.asarray(resA[i]["hTo"]), uH=uH[i], ab=np.asarray(resA[i]["fo"][1]),
                         gH=gH[i], cH=cH[i], ydT=np.ascontiguousarray(ydT_all[:, i * T:(i + 1) * T]),
                         invc=np.ascontiguousarray(np.broadcast_to(invc[None], (128, 4, 16)))))
    return run_bass_kernel_spmd(_prog("C"), maps, core_ids=list(range(NCORES))).results


def kernel(**inp):
    inp = {k: np.asarray(v) for k, v in inp.items()}
    x = inp["x"][0]
    xs = [x[i * T:(i + 1) * T] for i in range(NCORES)]
    for l in range(2):
        resA = run_A(xs, l, inp, _prog("A"))
        qkv = [np.concatenate([np.asarray(r["qkv"][j]) for r in resA], axis=1) for j in range(3)]
        ydT = run_ATT(qkv[0], qkv[1], qkv[2], _prog("ATT"))
        resC = run_C(resA, ydT, l, inp)
        xs = [np.asarray(r["xo"]) for r in resC]
    return np.concatenate(xs, axis=0)[None].astype(np.float32)
```

```python
import contextlib
import numpy as np
import ml_dtypes
import concourse.bass as bass
import concourse.mybir as mybir
from concourse.bass_utils import run_bass_kernel_spmd

F32 = mybir.dt.float32
BF16 = mybir.dt.bfloat16
AF = mybir.ActivationFunctionType
ALU = mybir.AluOpType
NPBF = ml_dtypes.bfloat16

NCORES = 8
D = 2048
S = 8192
T = S // NCORES
FF = 5632
W = 512
EPS = 1e-6
COMPUTE = ("pe", "act", "dve", "pool")


class Prog:
    def __init__(self, nc):
        self.nc = nc
        self.ops = []
        self.last_w = {}
        self.rd = {}

    def op(self, eng, fn, reads=(), writes=(), chan=None):
        deps = set()
        for k in reads:
            w = self.last_w.get(k)
            if w is not None:
                deps.add(w)
        for k in writes:
            w = self.last_w.get(k)
            if w is not None:
                deps.add(w)
            r = self.rd.get(k)
            if r:
                deps.update(r[0].values())
                deps.update(r[1])
        i = len(self.ops)
        self.ops.append((eng, fn, deps, chan))
        for k in reads:
            r = self.rd.setdefault(k, ({}, []))
            if chan is None:
                r[0][eng] = i
            else:
                r[1].append(i)
        for k in writes:
            self.last_w[k] = i
            self.rd[k] = ({}, [])
        return i

    def emit(self):
        nc, ops = self.nc, self.ops
        n = len(ops)
        sig = [False] * n
        for (_, _, deps, _) in ops:
            for d in deps:
                sig[d] = True
        cnt = {e: 0 for e in COMPUTE}
        val = [0] * n
        ccnt = {}
        per_eng = {e: [] for e in COMPUTE + ("sp",)}
        for i, (eng, fn, deps, chan) in enumerate(ops):
            per_eng[eng].append(i)
            if chan is not None:
                ccnt[chan] = ccnt.get(chan, 0) + 16
                val[i] = ccnt[chan]
            elif sig[i]:
                cnt[eng] += 1
                val[i] = cnt[eng]
        with contextlib.ExitStack() as es:
            sems = {e: es.enter_context(nc.semaphore("s_" + e)) for e in COMPUTE}
            csems = {}
            for j, c in enumerate(ccnt):
                csems[c] = es.enter_context(nc.semaphore("c%d" % j))
            block = es.enter_context(nc.Block())

            def make(name):
                def body(e):
                    waited = {}
                    for i in per_eng[name]:
                        eng, fn, deps, chan = ops[i]
                        need = {}
                        for d in deps:
                            de, _, _, dch = ops[d]
                            if dch is not None:
                                key, s = ("c", dch), csems[dch]
                            else:
                                if de == name and name == "pe":
                                    continue
                                key, s = ("e", de), sems[de]
                            if need.get(key, (0, None))[0] < val[d]:
                                need[key] = (val[d], s)
                        for key, (v, s) in need.items():
                            if waited.get(key, 0) < v:
                                e.wait_ge(s, v)
                                waited[key] = v
                        ins = fn(e)
                        if chan is not None:
                            ins.then_inc(csems[chan], 16)
                        elif sig[i]:
                            ins.then_inc(sems[eng], 1)
                    if name == "sp":
                        for c, tot in ccnt.items():
                            e.wait_ge(csems[c], tot)
                return body

            block.tensor(make("pe"))
            block.scalar(make("act"))
            block.vector(make("dve"))
            block.gpsimd(make("pool"))
            block.sync(make("sp"))


class Ctx:
    pass


def alloc_common(nc, es, P, xn_bufs=2):
    c = Ctx()
    c.nc, c.P = nc, P
    sb = lambda name, shape, dt: es.enter_context(nc.sbuf_tensor("sb_" + name, shape, dt))
    c.acc = sb("acc", [128, 8, D], F32)
    c.hT = sb("hT", [128, 16, T], BF16)
    c.wb = sb("wb", [128, 12, 2048], BF16)
    c.aT = sb("aT", [128, 2, 2, T], BF16)
    c.sg = sb("sg", [128, 2, T], F32)
    c.gb = sb("gb", [128, D], F32)
    c.xn = sb("xn", [128, xn_bufs, D], BF16)
    c.xn_bufs = xn_bufs
    c.ss = sb("ss", [128, 16], F32)
    c.rstd = sb("rstd", [128, 16], F32)
    c.ident = sb("ident", [128, 128], BF16)
    c.epsb = sb("epsb", [128, 2], F32)
    P.op("pool", lambda e: e.memset(c.epsb[:, 0:1], EPS), writes=["epsb"])
    P.op("pool", lambda e: e.memset(c.epsb[:, 1:2], 64.0 * EPS), writes=["epsb"])
    c.ps = es.enter_context(nc.psum_tensor("ps", [128, 7, 512], F32))
    c.pT = es.enter_context(nc.psum_tensor("pT", [128, 8, 128], BF16))
    c.dbank = 0
    return c


def load_const(c, tile_ap, key, src_ap):
    c.P.op("sp", lambda e: e.dma_start(out=tile_ap, in_=src_ap), writes=[key], chan=key)


def wload(c, slot, src_ap, shape3=None):
    dst = c.wb[:, slot, :]
    if shape3 is not None:
        dst = dst.rearrange("p (k c) -> p k c", k=shape3[0])
    c.P.op("pool", lambda e: e.dma_start(out=dst, in_=src_ap), writes=[("wb", slot)], chan=("wb", slot))


def wcols(w_ap, c0, kh):
    return w_ap[kh * 1024:(kh + 1) * 1024, c0:c0 + 256].rearrange("(k p) c -> p k c", p=128)


def rmsnorm_to_hT(c, gain_key):
    P = c.P
    for tt in range(8):
        P.op("act", lambda e, tt=tt: e.activation(out=c.sg[:, :, :].rearrange("p a t -> p (a t)"), in_=c.acc[:, tt, :], func=AF.Square,
                                                  accum_out=c.ss[:, tt:tt + 1]),
             reads=[("acc", tt)], writes=[("sg", 0), ("sg", 1), "ss"])
    P.op("act", lambda e: e.activation(out=c.rstd[:, 0:8], in_=c.ss[:, 0:8], func=AF.Sqrt, scale=1.0 / D, bias=c.epsb[:, 0:1]),
         reads=["ss", "epsb"], writes=["rstd"])
    P.op("dve", lambda e: e.reciprocal(out=c.rstd[:, 0:8], in_=c.rstd[:, 0:8]), reads=["rstd"], writes=["rstd"])
    for tt in range(8):
        b = tt % c.xn_bufs
        P.op("dve", lambda e, tt=tt, b=b: e.scalar_tensor_tensor(out=c.xn[:, b, :], in0=c.acc[:, tt, :], scalar=c.rstd[:, tt:tt + 1],
                                                                 in1=c.gb[:, :], op0=ALU.mult, op1=ALU.mult),
             reads=[("acc", tt), "rstd", gain_key], writes=[("xn", b)])
        for half in range(2):
            for kk in range(8):
                k = half * 8 + kk
                P.op("pe", lambda e, k=k, kk=kk, b=b: e.transpose(out=c.pT[:, kk, :], in_=c.xn[:, b, k * 128:(k + 1) * 128],
                                                                  identity=c.ident[:, :]),
                     reads=[("xn", b), "ident"], writes=["pT"])
            eng = "act" if half == 0 else "dve"
            if eng == "act":
                fn = lambda e, tt=tt, half=half: e.copy(out=c.hT[:, half * 8:(half + 1) * 8, tt * 128:(tt + 1) * 128], in_=c.pT[:, :, :])
            else:
                fn = lambda e, tt=tt, half=half: e.tensor_copy(out=c.hT[:, half * 8:(half + 1) * 8, tt * 128:(tt + 1) * 128], in_=c.pT[:, :, :])
            P.op(eng, fn, reads=["pT"], writes=[("hT", k, tt // 4) for k in range(half * 8, half * 8 + 8)])


def down_tiles(c, a_fn, wslots, tiles, nj=2):
    P = c.P
    for (tt8, dc) in tiles:
        bank = 4 + c.dbank % 3
        c.dbank += 1
        for j in range(nj):
            ap, key = a_fn(j, tt8)
            P.op("pe", lambda e, ap=ap, j=j, dc=dc, bank=bank: e.matmul(c.ps[:, bank, :], lhsT=ap,
                                                                         rhs=c.wb[:, wslots[j], dc * 512:(dc + 1) * 512],
                                                                         start=(j == 0), stop=(j == nj - 1)),
                 reads=[key, ("wb", wslots[j])], writes=[("ps", bank)])
        P.op("dve", lambda e, tt8=tt8, dc=dc, bank=bank: e.tensor_tensor(out=c.acc[:, tt8, dc * 512:(dc + 1) * 512], in0=c.ps[:, bank, :],
                                                                          in1=c.acc[:, tt8, dc * 512:(dc + 1) * 512], op=ALU.add),
             reads=[("ps", bank), ("acc", tt8)], writes=[("acc", tt8)])


def fm_matmul(c, slots, cc, banks, hook=None):
    P = c.P
    for k in range(16):
        if hook is not None and k % 2 == 0 and k > 0:
            hook()
        for tt in range(2):
            P.op("pe", lambda e, k=k, tt=tt: e.matmul(c.ps[:, banks[tt], :],
                                                       lhsT=c.wb[:, slots[k // 8], (k % 8) * 256 + cc * 128:(k % 8) * 256 + cc * 128 + 128],
                                                       rhs=c.hT[:, k, tt * 512:(tt + 1) * 512], start=(k == 0), stop=(k == 15)),
                 reads=[("wb", slots[k // 8]), ("hT", k, tt)], writes=[("ps", banks[tt])])


def ffn(c, gain_key, w1, w3, w2, F):
    P = c.P
    rmsnorm_to_hT(c, gain_key)
    NG = F // 256
    ALLT = [(t8, dc) for t8 in range(8) for dc in range(4)]

    def req13(g):
        p = (g % 2) * 6
        for kh in range(2):
            wload(c, p + kh, wcols(w1, g * 256, kh), (8, 256))
            wload(c, p + 2 + kh, wcols(w3, g * 256, kh), (8, 256))

    def req2(g):
        p = (g % 2) * 6
        for j in range(2):
            wload(c, p + 4 + j, w2[g * 256 + j * 128:g * 256 + (j + 1) * 128, :])

    req13(0)
    req2(0)
    for g in range(NG + 1):
        p = (g % 2) * 6
        pending = []
        if g >= 1:
            pg = g - 1
            pp_ = (pg % 2) * 6
            a_fn = lambda j, t8, pg=pg: (c.aT[:, pg % 2, j, t8 * 128:(t8 + 1) * 128], ("aT", pg % 2, j))
            pending = [(a_fn, (pp_ + 4, pp_ + 5), [t]) for t in ALLT]

        def hook():
            if pending:
                down_tiles(c, *pending.pop(0))

        if g < NG:
            if g + 1 < NG:
                req13(g + 1)
            for j in range(2):
                fm_matmul(c, (p, p + 1), j, (0, 1), hook)
                P.op("act", lambda e, j=j: e.activation(out=c.sg[:, j, :], in_=c.ps[:, 0:2, :].rearrange("p a t -> p (a t)"), func=AF.Silu),
                     reads=[("ps", 0), ("ps", 1)], writes=[("sg", j)])
                hook()
                fm_matmul(c, (p + 2, p + 3), j, (2, 3), hook)
                P.op("dve", lambda e, j=j, g=g: e.scalar_tensor_tensor(out=c.aT[:, g % 2, j, :], in0=c.ps[:, 2:4, :].rearrange("p a t -> p (a t)"),
                                                                        scalar=0.5, in1=c.sg[:, j, :], op0=ALU.mult, op1=ALU.mult),
                     reads=[("ps", 2), ("ps", 3), ("sg", j)], writes=[("aT", g % 2, j)])
                hook()
        while pending:
            down_tiles(c, *pending.pop(0))
        if g + 1 < NG:
            req2(g + 1)


def dram_in(nc, name, shape, dt=F32):
    return nc.dram_tensor(name, list(shape), dt, kind="ExternalInput").ap()


def dram_out(nc, name, shape, dt=F32):
    return nc.dram_tensor(name, list(shape), dt, kind="ExternalOutput").ap()


def build_A(stage=9):
    nc = bass.Bass("TRN2", target_bir_lowering=False)
    x = dram_in(nc, "x", [T, D])
    g1 = dram_in(nc, "g1", [128, D])
    gm = dram_in(nc, "gm", [128, D])
    w1 = dram_in(nc, "w1", [D, FF])
    w3 = dram_in(nc, "w3", [D, FF])
    w2 = dram_in(nc, "w2", [FF, D])
    win = dram_in(nc, "win", [D, 9 * W])
    qkg = dram_in(nc, "qkg", [128, 2])
    ident = dram_in(nc, "ident", [128, 128], BF16)
    bones = dram_in(nc, "bones", [128, 128], BF16)
    x1 = dram_out(nc, "x1", [T, D])
    hTo = dram_out(nc, "hTo", [D, T], BF16)
    fo = dram_out(nc, "fo", [4, W, T])
    qkv = dram_out(nc, "qkv", [3, W, T], BF16)
    P = Prog(nc)
    with contextlib.ExitStack() as es:
        c = alloc_common(nc, es, P)
        sb = lambda name, shape, dt: es.enter_context(nc.sbuf_tensor("sb_" + name, shape, dt))
        ost = sb("ost", [128, 2, T], F32)
        obf = sb("obf", [128, 2, T], BF16)
        sqb = sb("sqb", [128, T], BF16)
        rs = sb("rs", [128, T], F32)
        qkg_t = sb("qkg_t", [128, 2], F32)
        bones_t = sb("bones_t", [128, 128], BF16)
        load_const(c, c.ident[:, :], "ident", ident)
        load_const(c, bones_t[:, :], "bones", bones)
        load_const(c, qkg_t[:, :], "qkg", qkg)
        load_const(c, c.gb[:, :], "gb", g1)
        for tt in range(8):
            P.op("sp", lambda e, tt=tt: e.dma_start(out=c.acc[:, tt, :], in_=x[tt * 128:(tt + 1) * 128, :]),
                 writes=[("acc", tt)], chan=("acc", tt))
        if stage == 1:
            rmsnorm_to_hT(c, "gb")
            for k in range(16):
                P.op("sp", lambda e, k=k: e.dma_start(out=hTo[k * 128:(k + 1) * 128, :], in_=c.hT[:, k, :]),
                     reads=[("hT", k, 0), ("hT", k, 1)], chan=("hTo", k % 4))
            P.op("sp", lambda e: e.dma_start(out=x1[0:128, 0:16], in_=c.rstd[:, :]), reads=["rstd"], chan="dbg")
            P.op("sp", lambda e: e.dma_start(out=x1[0:128, 16:32], in_=c.ss[:, :]), reads=["ss"], chan="dbg")
            P.emit()
            return nc
        ffn(c, "gb", w1, w3, w2, FF)
        for tt in range(8):
            P.op("sp", lambda e, tt=tt: e.dma_start(out=x1[tt * 128:(tt + 1) * 128, :], in_=c.acc[:, tt, :]),
                 reads=[("acc", tt)], chan=("x1o", tt))
        P.op("sp", lambda e: e.dma_start(out=c.gb[:, :], in_=gm), writes=["gb"], chan="gb")
        rmsnorm_to_hT(c, "gb")
        for k in range(16):
            P.op("sp", lambda e, k=k: e.dma_start(out=hTo[k * 128:(k + 1) * 128, :], in_=c.hT[:, k, :]),
                 reads=[("hT", k, 0), ("hT", k, 1)], chan=("hTo", k % 4))
        tmp = c.acc[:, :, :].rearrange("p a d -> p (a d)")
        tview = lambda i: tmp[:, i * 4096:(i + 1) * 4096].rearrange("p (c t) -> p c t", c=4)
        tkeys = lambda i: [("acc", 2 * i), ("acc", 2 * i + 1)]
        tA, tB, tQ = tview(0), tview(1), tview(2)
        nst = [0]

        def store(dst_ap, src_tile_ap, key, chname):
            P.op("sp", lambda e: e.dma_start(out=dst_ap, in_=src_tile_ap), reads=[key], chan=chname)

        for g in range(18):
            slot_pair = ((g % 2) * 6, (g % 2) * 6 + 1)
            for kh in range(2):
                wload(c, slot_pair[kh], wcols(win, g * 256, kh), (8, 256))
            s = g // 2
            for j in range(2):
                cc = (g % 2) * 2 + j
                banks = (0, 1) if j == 0 else (2, 3)
                fm_matmul(c, slot_pair, j, banks)
                psv = c.ps[:, banks[0]:banks[0] + 2, :].rearrange("p a t -> p (a t)")
                pk = [("ps", banks[0]), ("ps", banks[1])]
                ob = nst[0] % 2
                nst[0] += 1
                rows = slice(cc * 128, (cc + 1) * 128)
                if s == 0:
                    P.op("act", lambda e, psv=psv, cc=cc: e.copy(out=tA[:, cc, :], in_=psv), reads=pk, writes=tkeys(0))
                elif s == 3:
                    P.op("act", lambda e, psv=psv, cc=cc: e.copy(out=tB[:, cc, :], in_=psv), reads=pk, writes=tkeys(1))
                elif s in (1, 5):
                    P.op("act", lambda e, psv=psv, ob=ob: e.copy(out=ost[:, ob, :], in_=psv), reads=pk, writes=[("ost", ob)])
                    store(fo[1 if s == 1 else 3, rows, :], ost[:, ob, :], ("ost", ob), ("osto", ob))
                elif s == 2:
                    P.op("dve", lambda e, psv=psv, ob=ob, cc=cc: e.tensor_tensor(out=ost[:, ob, :], in0=psv, in1=tA[:, cc, :], op=ALU.mult),
                         reads=pk + tkeys(0), writes=[("ost", ob)])
                    store(fo[0, rows, :], ost[:, ob, :], ("ost", ob), ("osto", ob))
                elif s == 4:
                    P.op("act", lambda e, psv=psv, j=j: e.activation(out=c.sg[:, j, :], in_=psv, func=AF.Sigmoid), reads=pk, writes=[("sg", j)])
                    P.op("dve", lambda e, ob=ob, cc=cc, j=j: e.tensor_tensor(out=ost[:, ob, :], in0=c.sg[:, j, :], in1=tB[:, cc, :], op=ALU.mult),
                         reads=[("sg", j)] + tkeys(1), writes=[("ost", ob)])
                    store(fo[2, rows, :], ost[:, ob, :], ("ost", ob), ("osto", ob))
                elif s in (6, 7):
                    P.op("act", lambda e, psv=psv, cc=cc: e.copy(out=tQ[:, cc, :], in_=psv), reads=pk, writes=tkeys(2))
                    P.op("act", lambda e, psv=psv: e.activation(out=sqb[:, :], in_=psv, func=AF.Square), reads=pk, writes=["sqb"])
                    for tt in range(2):
                        P.op("pe", lambda e, tt=tt: e.matmul(c.ps[:, 4 + tt, :], lhsT=bones_t[:, :], rhs=sqb[:, tt * 512:(tt + 1) * 512],
                                                              start=True, stop=True),
                             reads=["bones", "sqb"], writes=[("ps", 4 + tt)])
                    sc, bcol = (1.0, 1) if s == 6 else (1.0 / 64.0, 0)
                    P.op("act", lambda e, sc=sc, bcol=bcol: e.activation(out=rs[:, :], in_=c.ps[:, 4:6, :].rearrange("p a t -> p (a t)"),
                                                                         func=AF.Sqrt, scale=sc, bias=c.epsb[:, bcol:bcol + 1]),
                         reads=[("ps", 4), ("ps", 5), "epsb"], writes=["rs"])
                    P.op("dve", lambda e: e.reciprocal(out=rs[:, :], in_=rs[:, :]), reads=["rs"], writes=["rs"])
                    P.op("dve", lambda e, cc=cc, ob=ob, s=s: e.scalar_tensor_tensor(out=obf[:, ob, :], in0=tQ[:, cc, :],
                                                                                   scalar=qkg_t[:, s - 6:s - 5], in1=rs[:, :],
                                                                                   op0=ALU.mult, op1=ALU.mult),
                         reads=tkeys(2) + ["rs", "qkg"], writes=[("obf", ob)])
                    store(qkv[s - 6, rows, :], obf[:, ob, :], ("obf", ob), ("obfo", ob))
                else:
                    P.op("act", lambda e, psv=psv, ob=ob: e.copy(out=obf[:, ob, :], in_=psv), reads=pk, writes=[("obf", ob)])
                    store(qkv[2, rows, :], obf[:, ob, :], ("obf", ob), ("obfo", ob))
        P.emit()
    return nc


def _consts():
    ident = np.eye(128, dtype=np.float32).astype(NPBF)
    bones = np.kron(np.eye(2, dtype=np.float32), np.ones((64, 64), np.float32)).astype(NPBF)
    return ident, bones


def bcast(v):
    return np.ascontiguousarray(np.broadcast_to(np.asarray(v, np.float32)[None, :], (128, v.shape[0])))


def run_A(xs, l, inp, nc=None):
    nc = nc or build_A()
    ident, bones = _consts()
    qkg = np.stack([np.tile(inp["q_norm"][l], 2), np.tile(inp["k_norm"][l], 2)], axis=1).astype(np.float32)
    common = dict(g1=bcast(inp["ffn1_norm"][l]), gm=bcast(inp["mix_norm"][l]), w1=inp["ffn1_w1"][l], w3=inp["ffn1_w3"][l],
                  w2=inp["ffn1_w2"][l], win=inp["w_in"][l], qkg=np.ascontiguousarray(qkg), ident=ident, bones=bones)
    maps = [dict(common, x=np.ascontiguousarray(xs[i])) for i in range(NCORES)]
    return run_bass_kernel_spmd(nc, maps, core_ids=list(range(NCORES))).results


def mixer_thunks(P, mt, ps7, plb, ym, tl, cst, srcs, yabc):
    th = []
    add = lambda *a, **k: th.append(lambda: P.op(*a, **k))
    uH, ab, gH, cH = srcs
    cwa_t, cwb_t, lnp_t, invc_t, t16, onesf, pm_t, epsb = cst

    def tv(off, n):
        return mt[:, off:off + n], [("mt", i) for i in range(off // 2048, (off + n - 1) // 2048 + 1)]

    cbuf, cbk = tv(0, 4096)
    cbv = cbuf.rearrange("p (c t) -> p c t", c=4)
    gin = [tv(4096, 1056), tv(5152, 1056)]
    uin, uk = tv(6208, 1056)
    abin, abk = tv(7264, 1024)
    oa, oak = tv(8288, 1024)
    cin, cik = tv(9312, 1056)
    s_a, sak = tv(10368, 1056)
    s_b, sbk = tv(11424, 1056)
    pl, plk = tv(12480, 1024)
    sq, sqk = tv(13504, 1024)
    rsb, rsk = tv(14528, 1024)
    pk7 = [("ps", 7)]

    def out(br, ch):
        add("sp", lambda e, br=br, ch=ch: e.dma_start(out=yabc[br, ch * 128:(ch + 1) * 128, :], in_=ym[:, br, ch, :]),
            reads=[("ym", br, ch)], chan=("ymo", ch))

    for ch in range(4):
        rows = slice(ch * 128, (ch + 1) * 128)
        add("sp", lambda e, rows=rows: e.dma_start(out=uin[:, 0:2 + tl], in_=uH[rows, :]), writes=uk, chan="uin")
        add("sp", lambda e, rows=rows: e.dma_start(out=abin, in_=ab[rows, :]), writes=abk, chan="abin")
        add("dve", lambda e, ch=ch: e.tensor_scalar(out=oa, in0=uin[:, 0:tl], scalar1=cwa_t[:, ch, 0:1], scalar2=None, op0=ALU.mult),
            reads=uk + ["cwa"], writes=oak)
        for k in (1, 2):
            add("dve", lambda e, ch=ch, k=k: e.scalar_tensor_tensor(out=oa, in0=uin[:, k:k + tl], scalar=cwa_t[:, ch, k:k + 1], in1=oa,
                                                                    op0=ALU.mult, op1=ALU.add),
                reads=uk + oak + ["cwa"], writes=oak)
        add("dve", lambda e, ch=ch: e.tensor_tensor(out=ym[:, 0, ch, :], in0=oa, in1=abin, op=ALU.mult),
            reads=oak + abk, writes=[("ym", 0, ch)])
        out(0, ch)
    L = 16 + tl
    for g in range(4):
        rows = slice(g * 128, (g + 1) * 128)
        add("sp", lambda e, rows=rows: e.dma_start(out=cin[:, 0:L], in_=cH[rows, :]), writes=cik, chan="cin")
        steps = [(s_a, sak, cin, cik, 1), (s_b, sbk, s_a, sak, 2), (s_a, sak, s_b, sbk, 4), (s_b, sbk, s_a, sak, 8)]
        for (dst, dk, src, sk, sh) in steps[:g + 1]:
            lo = 2 * sh - 1
            add("dve", lambda e, dst=dst, src=src, sh=sh, lo=lo: e.tensor_tensor(out=dst[:, lo:L], in0=src[:, lo:L], in1=src[:, lo - sh:L - sh],
                                                                                 op=ALU.add),
                reads=sk, writes=dk)
        win, wk = (s_a, sak) if g % 2 == 0 else (s_b, sbk)
        wdt = float(2 ** (g + 1))
        add("dve", lambda e, win=win, wdt=wdt: e.scalar_tensor_tensor(out=pl, in0=win[:, 16:L], scalar=1.0 / wdt, in1=cin[:, 16:L],
                                                                      op0=ALU.mult, op1=ALU.subtract),
            reads=wk + cik, writes=plk)
        add("dve", lambda e, win=win, g=g: e.tensor_tensor(out=t16[:, :], in0=win[:, 16:32], in1=invc_t[:, g, :], op=ALU.mult),
            reads=wk + ["invc"], writes=["t16"])
        add("dve", lambda e: e.tensor_tensor(out=pl[:, 0:16], in0=t16[:, :], in1=cin[:, 16:32], op=ALU.subtract),
            reads=["t16"] + cik + plk, writes=plk)
        add("dve", lambda e: e.tensor_copy(out=plb, in_=pl), reads=plk, writes=["plb"])
        for tt in range(2):
            add("pe", lambda e, g=g, tt=tt: e.matmul(ps7, lhsT=pm_t[:, g, :], rhs=plb[:, tt * 512:(tt + 1) * 512], start=True, stop=True),
                reads=["pm", "plb"], writes=pk7)
            add("dve", lambda e, g=g, tt=tt: e.tensor_scalar(out=ym[:, 2, g, tt * 512:(tt + 1) * 512], in0=ps7, scalar1=lnp_t[:, 2, g:g + 1],
                                                             scalar2=None, op0=ALU.mult),
                reads=pk7 + ["lnp"], writes=[("ym", 2, g)])
        out(2, g)
    for ch in range(4):
        rows = slice(ch * 128, (ch + 1) * 128)
        gi, gk = gin[ch % 2]
        add("sp", lambda e, rows=rows, gi=gi: e.dma_start(out=gi[:, 0:30 + tl], in_=gH[rows, :]), writes=gk, chan=("gin", ch % 2))
        ck = [("mt", ch // 2)]
        add("dve", lambda e, ch=ch, gi=gi: e.tensor_scalar(out=cbv[:, ch, :], in0=gi[:, 0:tl], scalar1=cwb_t[:, ch, 0:1], scalar2=None, op0=ALU.mult),
            reads=gk + ["cwb"], writes=ck)
        for k in range(1, 31):
            add("dve", lambda e, ch=ch, gi=gi, k=k: e.scalar_tensor_tensor(out=cbv[:, ch, :], in0=gi[:, k:k + tl], scalar=cwb_t[:, ch, k:k + 1],
                                                                          in1=cbv[:, ch, :], op0=ALU.mult, op1=ALU.add),
                reads=gk + ck + ["cwb"], writes=ck)
    for tt in range(2):
        ts_ = slice(tt * 512, (tt + 1) * 512)
        for ch in range(4):
            add("pe", lambda e, ts_=ts_, ch=ch: e.matmul(ps7, lhsT=onesf[:, :], rhs=cbv[:, ch, ts_], start=(ch == 0), stop=(ch == 3)),
                reads=["onesf"] + cbk, writes=pk7)
        for ch in range(4):
            add("dve", lambda e, ts_=ts_, ch=ch: e.tensor_tensor(out=cbv[:, ch, ts_], in0=cbv[:, ch, ts_], in1=ps7, op=ALU.subtract),
                reads=pk7 + cbk, writes=cbk)
    for tt in range(2):
        ts_ = slice(tt * 512, (tt + 1) * 512)
        for ch in range(4):
            add("act", lambda e, ts_=ts_, ch=ch: e.activation(out=sq[:, 0:512], in_=cbv[:, ch, ts_], func=AF.Square), reads=cbk, writes=sqk)
            add("pe", lambda e, ch=ch: e.matmul(ps7, lhsT=onesf[:, :], rhs=sq[:, 0:512], start=(ch == 0), stop=(ch == 3)),
                reads=["onesf"] + sqk, writes=pk7)
        add("act", lambda e, ts_=ts_: e.activation(out=rsb[:, ts_], in_=ps7, func=AF.Ln, bias=epsb[:, 0:1]), reads=pk7 + ["epsb"], writes=rsk)
    add("act", lambda e: e.activation(out=rsb, in_=rsb, func=AF.Exp, scale=-0.5), reads=rsk, writes=rsk)
    for ch in range(4):
        add("dve", lambda e, ch=ch: e.tensor_tensor(out=cbv[:, ch, :], in0=cbv[:, ch, :], in1=rsb, op=ALU.mult), reads=cbk + rsk, writes=cbk)
        add("dve", lambda e, ch=ch: e.tensor_scalar(out=cbv[:, ch, :], in0=cbv[:, ch, :], scalar1=lnp_t[:, 0, ch:ch + 1], scalar2=lnp_t[:, 1, ch:ch + 1],
                                                    op0=ALU.mult, op1=ALU.add),
            reads=cbk + ["lnp"], writes=cbk)
        add("act", lambda e, ch=ch: e.activation(out=sq, in_=cbv[:, ch, :], func=AF.Exp, scale=-1.0), reads=cbk, writes=sqk)
        add("dve", lambda e: e.tensor_scalar(out=sq, in0=sq, scalar1=1.0, scalar2=None, op0=ALU.add), reads=sqk, writes=sqk)
        add("dve", lambda e: e.reciprocal(out=sq, in_=sq), reads=sqk, writes=sqk)
        add("dve", lambda e, ch=ch: e.tensor_tensor(out=ym[:, 1, ch, :], in0=cbv[:, ch, :], in1=sq, op=ALU.mult),
            reads=cbk + sqk, writes=[("ym", 1, ch)])
        out(1, ch)
    return th


def build_ATT(nq=16):
    nc = bass.Bass("TRN2", target_bir_lowering=False)
    qT = dram_in(nc, "qT", [64, S], BF16)
    kT = dram_in(nc, "kT", [64, S], BF16)
    v = dram_in(nc, "v", [128, 64, 64], BF16)
    negui = dram_in(nc, "negui", [128, 128], BF16)
    negone = dram_in(nc, "negone", [128, 128], BF16)
    ident = dram_in(nc, "ident", [128, 128], BF16)
    maska = dram_in(nc, "maska", [128, 4, 512], BF16)
    maskb = dram_in(nc, "maskb", [128, 4, 512], BF16)
    yT = dram_out(nc, "yT", [64, S], BF16)
    uH = dram_in(nc, "uH", [W, 2 + T])
    ab = dram_in(nc, "ab", [W, T])
    gH = dram_in(nc, "gH", [W, 30 + T])
    cH = dram_in(nc, "cH", [W, 16 + T])
    cwa = dram_in(nc, "cwa", [128, 4, 3])
    cwb = dram_in(nc, "cwb", [128, 4, 31])
    lnp = dram_in(nc, "lnp", [128, 3, 4])
    invc = dram_in(nc, "invc", [128, 4, 16])
    pmap = dram_in(nc, "pmap", [4, 128, 128])
    yabc = dram_out(nc, "yabc", [3, W, T], BF16)
    P = Prog(nc)
    with contextlib.ExitStack() as es:
        sb = lambda name, shape, dt: es.enter_context(nc.sbuf_tensor("sb_" + name, shape, dt))
        mt = sb("mt", [128, 16384], F32)
        ym = sb("ym", [128, 3, 4, T], BF16)
        plb = sb("plb", [128, T], BF16)
        pm_t = sb("pm", [128, 4, 128], BF16)
        cwa_t = sb("cwa", [128, 4, 3], F32)
        cwb_t = sb("cwb", [128, 4, 31], F32)
        lnp_t = sb("lnp", [128, 3, 4], F32)
        invc_t = sb("invc", [128, 4, 16], F32)
        t16 = sb("t16", [128, 16], F32)
        onesf = sb("onesf", [128, 128], F32)
        epsb = sb("epsb", [128, 1], F32)
        P.op("pool", lambda e: e.memset(onesf[:, :], 1.0 / W), writes=["onesf"])
        P.op("pool", lambda e: e.memset(epsb[:, :], EPS), writes=["epsb"])
        P.op("pool", lambda e: e.dma_start(out=pm_t[:, :, :], in_=pmap.rearrange("g c e -> c g e")), writes=["pm"], chan="pm")
        for key_, t__, src_ in (("cwa", cwa_t, cwa), ("cwb", cwb_t, cwb), ("lnp", lnp_t, lnp), ("invc", invc_t, invc)):
            P.op("sp", lambda e, t__=t__, src_=src_: e.dma_start(out=t__.ap(), in_=src_), writes=[key_], chan=key_)
        q_t = sb("q", [64, S], BF16)
        k_t = sb("k", [64, S], BF16)
        v_t = sb("v", [128, 64, 64], BF16)
        ui_t = sb("ui", [128, 128], BF16)
        on_t = sb("on", [128, 128], BF16)
        id_t = sb("id", [128, 128], BF16)
        mab_t = sb("mab", [128, 4, 512], BF16)
        mb_t = sb("mb", [128, 4, 512], BF16)
        NB = 3
        e1 = sb("e1", [128, 2, 2, 512], F32)
        spb = sb("spb", [128, NB, 2, 512], BF16)
        lsum = sb("lsum", [128, 512], F32)
        lb = sb("lb", [128, NB, 512], BF16)
        at = sb("at", [128, 2, 2, 512], BF16)
        yo = sb("yo", [64, 2, 512], BF16)
        ps = es.enter_context(nc.psum_tensor("ps", [128, 8, 512], F32))
        cmap = [("q", q_t, qT), ("k", k_t, kT), ("v", v_t, v), ("ui", ui_t, negui), ("on", on_t, negone), ("id", id_t, ident),
                ("mab", mab_t, maska), ("mb", mb_t, maskb)]
        for key, t_, src in cmap:
            if key in ("q", "k"):
                for h in range(4):
                    P.op("sp", lambda e, t_=t_, src=src, h=h: e.dma_start(out=t_[:, h * 2048:(h + 1) * 2048], in_=src[:, h * 2048:(h + 1) * 2048]),
                         writes=[(key, h)], chan=(key, h))
            else:
                P.op("sp", lambda e, t_=t_, src=src: e.dma_start(out=t_.ap(), in_=src), writes=[key], chan=key)
        pairs = []
        for qi in range(nq):
            npair = 2 * qi + 2
            for pi in range(npair):
                pairs.append((qi, pi, npair))

        def stage1(i):
            qi, pi, npair = pairs[i]
            pb = npair - 1 - pi
            zs = i % 2
            b3 = i % NB
            t0 = qi * 512
            for h in range(2):
                kb = 2 * pb + 1 - h
                P.op("pe", lambda e, kb=kb, h=h: e.matmul(ps[:, h, :], lhsT=k_t[:, kb * 128:(kb + 1) * 128], rhs=q_t[:, t0:t0 + 512],
                                                         start=True, stop=True),
                     reads=[("k", kb // 16), ("q", qi // 4)], writes=[("ps", h)])
            pz = [("ps", 0), ("ps", 1)]
            P.op("act", lambda e: e.activation(out=e1[:, zs, :, :].rearrange("p a t -> p (a t)"),
                                               in_=ps[:, 0:2, :].rearrange("p a t -> p (a t)"), func=AF.Exp),
                 reads=pz, writes=[("e1", zs)])
            P.op("act", lambda e: e.activation(out=spb[:, b3, :, :].rearrange("p a t -> p (a t)"),
                                               in_=e1[:, zs, :, :].rearrange("p a t -> p (a t)"), func=AF.Ln, bias=1.0),
                 reads=[("e1", zs)], writes=[("spb", b3)])
            if pi < 2:
                r0 = 3 - 2 * pi
                P.op("dve", lambda e: e.tensor_tensor(out=spb[:, b3, :, :], in0=spb[:, b3, :, :], in1=mab_t[:, 3 - r0:5 - r0, :], op=ALU.mult),
                     reads=[("spb", b3), "mab"], writes=[("spb", b3)])
            if pi + 1 < npair:
                if pi == 0:
                    P.op("dve", lambda e: e.tensor_tensor(out=lsum[:, :], in0=spb[:, b3, 0, :], in1=spb[:, b3, 1, :], op=ALU.add),
                         reads=[("spb", b3)], writes=["lsum"])
                else:
                    P.op("dve", lambda e: e.tensor_tensor(out=lsum[:, :], in0=lsum[:, :], in1=spb[:, b3, 0, :], op=ALU.add),
                         reads=[("spb", b3), "lsum"], writes=["lsum"])
                    P.op("dve", lambda e: e.tensor_tensor(out=lsum[:, :], in0=lsum[:, :], in1=spb[:, b3, 1, :], op=ALU.add),
                         reads=[("spb", b3), "lsum"], writes=["lsum"])
                nb3 = (i + 1) % NB
                P.op("dve", lambda e: e.tensor_copy(out=lb[:, nb3, :], in_=lsum[:, :]), reads=["lsum"], writes=[("lb", nb3)])

        def stage2(i):
            qi, pi, npair = pairs[i]
            pb = npair - 1 - pi
            b3 = i % NB
            t0 = qi * 512
            a2 = i % 2
            bb = 2 + 2 * (i % 2)
            for h in range(2):
                kb = 2 * pb + 1 - h
                bank = bb + h
                mm = [(k_t[:, kb * 128:(kb + 1) * 128], q_t[:, t0:t0 + 512], [("k", kb // 16), ("q", qi // 4)]),
                      (ui_t[:, :], spb[:, b3, h, :], ["ui", ("spb", b3)])]
                if h == 1:
                    mm.append((on_t[:, :], spb[:, b3, 0, :], ["on", ("spb", b3)]))
                if pi > 0:
                    mm.append((on_t[:, :], lb[:, b3, :], ["on", ("lb", b3)]))
                if pi < 2:
                    r = 3 - 2 * pi - h
                    mm.append((id_t[:, :], mb_t[:, r, :], ["id", "mb"]))
                for j, (l_, r_, rk) in enumerate(mm):
                    P.op("pe", lambda e, l_=l_, r_=r_, j=j, n=len(mm), bank=bank: e.matmul(ps[:, bank, :], lhsT=l_, rhs=r_, start=(j == 0),
                                                                                         stop=(j == n - 1)),
                         reads=rk, writes=[("ps", bank)])
            P.op("act", lambda e: e.activation(out=at[:, a2, :, :].rearrange("p a t -> p (a t)"),
                                               in_=ps[:, bb:bb + 2, :].rearrange("p a t -> p (a t)"), func=AF.Exp),
                 reads=[("ps", bb), ("ps", bb + 1)], writes=[("at", a2)])

        def stage3(i):
            qi, pi, npair = pairs[i]
            pb = npair - 1 - pi
            t0 = qi * 512
            a2 = i % 2
            ob = 6
            for h in range(2):
                kb = 2 * pb + 1 - h
                first = (pi == 0 and h == 0)
                last = (pi == npair - 1 and h == 1)
                P.op("pe", lambda e, kb=kb, h=h, first=first, last=last: e.matmul(ps[0:64, ob, :], lhsT=v_t[:, kb, :], rhs=at[:, a2, h, :],
                                                                                 start=first, stop=last),
                     reads=["v", ("at", a2)], writes=[("ps", ob)])
            if pi == npair - 1:
                P.op("dve", lambda e: e.tensor_copy(out=yo[:, qi % 2, :], in_=ps[0:64, ob, :]), reads=[("ps", ob)], writes=[("yo", qi % 2)])
                P.op("sp", lambda e: e.dma_start(out=yT[:, t0:t0 + 512], in_=yo[:, qi % 2, :]), reads=[("yo", qi % 2)], chan=("yo", qi % 2))

        n = len(pairs)
        mth = mixer_thunks(P, mt[:, :], ps[:, 7, :], plb[:, :], ym, T, (cwa_t, cwb_t, lnp_t, invc_t, t16, onesf, pm_t, epsb),
                           (uH, ab, gH, cH), yabc) if nq == 16 else []
        per = 1
        for i in range(n + 2):
            if i < n:
                stage1(i)
            if 1 <= i <= n:
                stage2(i - 1)
            if i >= 2:
                stage3(i - 2)
            for _ in range(per):
                if mth:
                    mth.pop(0)()
        while mth:
            mth.pop(0)()
        P.emit()
    return nc


def att_consts():
    j = np.arange(128)
    negui = -(j[:, None] >= j[None, :]).astype(np.float32)
    negone = -np.ones((128, 128), np.float32)
    tl = np.arange(512)
    ma = np.stack([(tl[None, :] > (r * 128 + j[:, None])).astype(np.float32) for r in range(4)], axis=1)
    mb = (1.0 - ma) * -30000.0
    return dict(negui=negui.astype(NPBF), negone=negone.astype(NPBF), ident=np.eye(128, dtype=np.float32).astype(NPBF),
                maska=np.ascontiguousarray(ma[:, ::-1, :]).astype(NPBF), maskb=mb.astype(NPBF))


def mixer_inputs(resA, l, inp):
    uH = halo([np.asarray(r["fo"][0]) for r in resA], 2)
    gH = halo([np.asarray(r["fo"][2]) for r in resA], 30)
    cH = halo([np.asarray(r["fo"][3]) for r in resA], 16)
    cwa = np.ascontiguousarray(inp["conv_a"][l].reshape(3, 4, 128).transpose(2, 1, 0))
    cwb = np.ascontiguousarray(inp["conv_b"][l].reshape(31, 4, 128).transpose(2, 1, 0))
    lnp = np.ascontiguousarray(np.stack([pp(inp["ln_b_gain"][l]), pp(inp["ln_b_bias"][l]), pp(inp["pool_scale"][l])], axis=1))
    out = []
    for i in range(NCORES):
        pos = np.arange(16) + i * T
        invc = np.stack([1.0 / np.minimum(pos + 1, 2 ** (g + 1)) for g in range(4)]).astype(np.float32)
        out.append(dict(uH=uH[i], ab=np.asarray(resA[i]["fo"][1]), gH=gH[i], cH=cH[i], cwa=cwa, cwb=cwb, lnp=lnp, pmap=inp["pool_map"][l],
                        invc=np.ascontiguousarray(np.broadcast_to(invc[None], (128, 4, 16)))))
    return out


def run_ATT(qT_all, kT_all, vT_all, mix_in, nc=None):
    nc = nc or build_ATT()
    cst = att_consts()
    maps = []
    for h in range(NCORES):
        rows = slice(h * 64, (h + 1) * 64)
        vh = np.ascontiguousarray(vT_all[rows, :].T.reshape(64, 128, 64).transpose(1, 0, 2))
        maps.append(dict(cst, qT=np.ascontiguousarray(qT_all[rows]), kT=np.ascontiguousarray(kT_all[rows]), v=vh, **mix_in[h]))
    res = run_bass_kernel_spmd(nc, maps, core_ids=list(range(NCORES))).results
    return np.concatenate([np.asarray(r["yT"]) for r in res], axis=0), [np.asarray(r["yabc"]) for r in res]


class WStream:
    def __init__(self, c, items, R=12):
        self.c, self.items, self.R = c, items, R
        self.next = 0
        self.released = set()
        self._try()

    def _try(self):
        while self.next < len(self.items) and (self.next < self.R or (self.next - self.R) in self.released):
            src, shape3 = self.items[self.next]
            wload(self.c, self.next % self.R, src, shape3)
            self.next += 1

    def slot(self, n):
        assert n < self.next, (n, self.next)
        return n % self.R

    def release(self, n):
        self.released.add(n)
        self._try()


def build_C():
    nc = bass.Bass("TRN2", target_bir_lowering=False)
    x1 = dram_in(nc, "x1", [T, D])
    hTi = dram_in(nc, "hTi", [D, T], BF16)
    yin = dram_in(nc, "yin", [4, W, T], BF16)
    wbr = dram_in(nc, "wbr", [4, W, D])
    wg = dram_in(nc, "wg", [D, 4 * D])
    wo = dram_in(nc, "wo", [D, D])
    g2 = dram_in(nc, "g2", [128, D])
    w1 = dram_in(nc, "w1", [D, FF])
    w3 = dram_in(nc, "w3", [D, FF])
    w2 = dram_in(nc, "w2", [FF, D])
    ident = dram_in(nc, "ident", [128, 128], BF16)
    xo = dram_out(nc, "xo", [T, D])
    P = Prog(nc)
    with contextlib.ExitStack() as es:
        c = alloc_common(nc, es, P, xn_bufs=1)
        sb = lambda name, shape, dt: es.enter_context(nc.sbuf_tensor("sb_" + name, shape, dt))
        yT = sb("yT", [128, 4, 4, T], BF16)
        load_const(c, c.ident[:, :], "ident", ident)
        for k in range(16):
            P.op("sp", lambda e, k=k: e.dma_start(out=c.hT[:, k, :], in_=hTi[k * 128:(k + 1) * 128, :]),
                 writes=[("hT", k, 0), ("hT", k, 1)], chan=("hTi", k % 4))
        for br in range(4):
            for ch in range(4):
                P.op("sp", lambda e, br=br, ch=ch: e.dma_start(out=yT[:, br, ch, :], in_=yin[br, ch * 128:(ch + 1) * 128, :]),
                     writes=[("yT", br, ch)], chan=("yin", ch))
        for tt in range(8):
            P.op("sp", lambda e, tt=tt: e.dma_start(out=c.acc[:, tt, :], in_=x1[tt * 128:(tt + 1) * 128, :]),
                 writes=[("acc", tt)], chan=("acc", tt))
        items = []
        for dc in range(16):
            for i in range(4):
                items.append((wg[:, i * D + dc * 128:i * D + (dc + 1) * 128].rearrange("(k p) c -> p k c", p=128), (16, 128)))
            items.append((wbr[:, :, dc * 128:(dc + 1) * 128].rearrange("i (k p) c -> p (i k) c", p=128), (16, 128)))
        ws = WStream(c, items, R=8)
        mf = c.gb[:, 0:T]
        ptmp = c.gb[:, T:2 * T]
        ALLT = [(t8, dq) for t8 in range(8) for dq in range(4)]
        pending = []
        for dc in range(16):
            j, cc = dc // 2, dc % 2
            base = dc * 5
            wload(c, 8 + (j % 2) * 2 + cc, wo[dc * 128:(dc + 1) * 128, :])
            for i in range(4):
                sl = ws.slot(base + i)
                for k in range(16):
                    if pending and k % 4 == 2:
                        down_tiles(c, *pending.pop(0))
                    for tt in range(2):
                        P.op("pe", lambda e, sl=sl, k=k, tt=tt: e.matmul(c.ps[:, tt, :], lhsT=c.wb[:, sl, k * 128:(k + 1) * 128],
                                                                         rhs=c.hT[:, k, tt * 512:(tt + 1) * 512], start=(k == 0), stop=(k == 15)),
                             reads=[("wb", sl), ("hT", k, tt)], writes=[("ps", tt)])
                ws.release(base + i)
                sgb = i % 2
                P.op("act", lambda e, sgb=sgb: e.activation(out=c.sg[:, sgb, :], in_=c.ps[:, 0:2, :].rearrange("p a t -> p (a t)"), func=AF.Sigmoid),
                     reads=[("ps", 0), ("ps", 1)], writes=[("sg", sgb)])
                slb = ws.slot(base + 4)
                for k in range(4):
                    for tt in range(2):
                        P.op("pe", lambda e, slb=slb, i=i, k=k, tt=tt: e.matmul(c.ps[:, 2 + tt, :], lhsT=c.wb[:, slb, (i * 4 + k) * 128:(i * 4 + k + 1) * 128],
                                                                                rhs=yT[:, i, k, tt * 512:(tt + 1) * 512], start=(k == 0), stop=(k == 3)),
                             reads=[("wb", slb), ("yT", i, k)], writes=[("ps", 2 + tt)])
                if i == 3:
                    ws.release(base + 4)
                dst, dk = (mf, "mf") if i == 0 else (ptmp, "ptmp")
                P.op("dve", lambda e, dst=dst, sgb=sgb: e.tensor_tensor(out=dst, in0=c.ps[:, 2:4, :].rearrange("p a t -> p (a t)"), in1=c.sg[:, sgb, :],
                                                                        op=ALU.mult),
                     reads=[("ps", 2), ("ps", 3), ("sg", sgb)], writes=[dk])
                if i > 0:
                    P.op("dve", lambda e: e.tensor_tensor(out=mf, in0=mf, in1=ptmp, op=ALU.add), reads=["mf", "ptmp"], writes=["mf"])
            P.op("act", lambda e, j=j, cc=cc: e.copy(out=c.aT[:, j % 2, cc, :], in_=mf), reads=["mf"], writes=[("aT", j % 2, cc)])
            if cc == 1:
                while pending:
                    down_tiles(c, *pending.pop(0))
                wsl = (8 + (j % 2) * 2, 8 + (j % 2) * 2 + 1)
                a_fn = lambda jj, t8, j=j: (c.aT[:, j % 2, jj, t8 * 128:(t8 + 1) * 128], ("aT", j % 2, jj))
                pending = [(a_fn, wsl, [t]) for t in ALLT]
        while pending:
            down_tiles(c, *pending.pop(0))
        P.op("sp", lambda e: e.dma_start(out=c.gb[:, :], in_=g2), reads=["mf", "ptmp"], writes=["gb", "mf", "ptmp"], chan="gb")
        ffn(c, "gb", w1, w3, w2, FF)
        for tt in range(8):
            P.op("sp", lambda e, tt=tt: e.dma_start(out=xo[tt * 128:(tt + 1) * 128, :], in_=c.acc[:, tt, :]),
                 reads=[("acc", tt)], chan=("xo", tt))
        P.emit()
    return nc


_NC = {}


def _prog(name):
    if name not in _NC:
        _NC[name] = {"A": build_A, "ATT": build_ATT, "C": build_C}[name]()
    return _NC[name]


def halo(parts, n):
    out = []
    for i in range(NCORES):
        prev = parts[i - 1][:, T - n:] if i > 0 else np.zeros((parts[0].shape[0], n), parts[0].dtype)
        out.append(np.ascontiguousarray(np.concatenate([prev, parts[i]], axis=1)))
    return out


def pp(v):
    return np.ascontiguousarray(np.asarray(v, np.float32).reshape(4, 128).T)


def run_C(resA, ydT_all, yabc, l, inp):
    ident, _ = _consts()
    common = dict(wbr=inp["w_branch"][l], wg=inp["w_gate"][l], wo=inp["w_out"][l],
                  g2=bcast(inp["ffn2_norm"][l]), w1=inp["ffn2_w1"][l], w3=inp["ffn2_w3"][l], w2=inp["ffn2_w2"][l], ident=ident)
    maps = []
    for i in range(NCORES):
        yin = np.ascontiguousarray(np.concatenate([yabc[i], ydT_all[None, :, i * T:(i + 1) * T]], axis=0))
        maps.append(dict(common, x1=np.asarray(resA[i]["x1"]), hTi=np.asarray(resA[i]["hTo"]), yin=yin))
    return run_bass_kernel_spmd(_prog("C"), maps, core_ids=list(range(NCORES))).results


def kernel(**inp):
    inp = {k: np.asarray(v) for k, v in inp.items()}
    x = inp["x"][0]
    xs = [x[i * T:(i + 1) * T] for i in range(NCORES)]
    for l in range(2):
        resA = run_A(xs, l, inp, _prog("A"))
        qkv = [np.concatenate([np.asarray(r["qkv"][j]) for r in resA], axis=1) for j in range(3)]
        ydT, yabc = run_ATT(qkv[0], qkv[1], qkv[2], mixer_inputs(resA, l, inp), _prog("ATT"))
        resC = run_C(resA, ydT, yabc, l, inp)
        xs = [np.asarray(r["xo"]) for r in resC]
    return np.concatenate(xs, axis=0)[None].astype(np.float32)
```
